# Optimizing a Trainium2 kernel written in Bass

```python
import math
import jax
import jax.numpy as jnp
from jax import lax
import numpy as np

D_MODEL = 1024
BATCH = 2
SEQ = 8192
DEPTH = 2

NORM_EPS = 1e-6
POOL_WINDOWS = (2, 4, 8, 16)
POOL_GROUPS = 4
POOL_WIDTH = D_MODEL // 2
POOL_GROUP_DIM = POOL_WIDTH // POOL_GROUPS
ATTN_HEADS = 8
ATTN_HEAD_DIM = 64
ATTN_WIDTH = ATTN_HEADS * ATTN_HEAD_DIM
Q_BLOCK = 128
SSD_HEAD_DIM = 64
SSD_WIDTH = D_MODEL
SSD_HEADS = SSD_WIDTH // SSD_HEAD_DIM
SSD_GROUPS = 2
SSD_STATE = 128
SSD_CONV = 4
SSD_CHUNK = 128
SSD_CONV_CH = SSD_WIDTH + 2 * SSD_GROUPS * SSD_STATE
N_BRANCH = 3
FFN_DIM = 2816
FFN_CONV = 3
IN_SPLITS = (POOL_WIDTH, ATTN_WIDTH, ATTN_WIDTH, ATTN_WIDTH, ATTN_HEADS,
             SSD_WIDTH, SSD_CONV_CH, SSD_HEADS, N_BRANCH * D_MODEL)
IN_TOTAL = sum(IN_SPLITS)

kernel_name = "hybrid_pool_fox_ssd_block"


def rmsnorm(x, w):
    xf = x.astype(jnp.float32)
    y = xf * lax.rsqrt(jnp.mean(xf * xf, axis=-1, keepdims=True) + NORM_EPS)
    return (y * w.astype(jnp.float32)).astype(x.dtype)


def split_columns(proj):
    offsets, acc = [], 0
    for n in IN_SPLITS[:-1]:
        acc += n
        offsets.append(acc)
    return jnp.split(proj, offsets, axis=-1)


def causal_dwconv(x, w, b):
    K, C = w.shape
    y = lax.conv_general_dilated(
        x, w[:, None, :].astype(x.dtype), window_strides=(1,),
        padding=[(K - 1, 0)], dimension_numbers=('NWC', 'WIO', 'NWC'),
        feature_group_count=C)
    return y + b.astype(x.dtype)


def pool_mixer(v, mix_w, scale):
    B_, S, _ = v.shape
    vf = v.astype(jnp.float32).reshape(B_, S, POOL_GROUPS, POOL_GROUP_DIM)
    cs = jnp.cumsum(vf, axis=1)
    t = jnp.arange(S)
    pooled = []
    for g, w in enumerate(POOL_WINDOWS):
        c = cs[:, :, g]
        lag = jnp.pad(c, ((0, 0), (w, 0), (0, 0)))[:, :S]
        cnt = jnp.minimum(t + 1, w).astype(jnp.float32)[None, :, None]
        pooled.append((c - lag) / cnt)
    d = (jnp.stack(pooled, axis=2) - vf).astype(v.dtype)
    y = jnp.einsum('bsgc,gcd->bsgd', d, mix_w)
    return y.reshape(B_, S, POOL_WIDTH) * scale


def forgetting_attention(q, k, v, f_logit, f_bias):
    B_, S, _ = q.shape
    H, Dh = ATTN_HEADS, ATTN_HEAD_DIM
    q = q.reshape(B_, S, H, Dh).transpose(0, 2, 1, 3)
    k = k.reshape(B_, S, H, Dh).transpose(0, 2, 1, 3)
    v = v.reshape(B_, S, H, Dh).transpose(0, 2, 1, 3)
    logf = jax.nn.log_sigmoid((f_logit + f_bias).astype(jnp.float32))
    c = jnp.cumsum(logf, axis=1).transpose(0, 2, 1)
    nb = S // Q_BLOCK
    qb = q.reshape(B_, H, nb, Q_BLOCK, Dh).transpose(2, 0, 1, 3, 4)
    cb = c.reshape(B_, H, nb, Q_BLOCK).transpose(2, 0, 1, 3)
    kpos = jnp.arange(S)
    scale = Dh ** -0.5

    def block(args):
        qi, ci, i = args
        qpos = i * Q_BLOCK + jnp.arange(Q_BLOCK)
        s = jnp.einsum('bhqd,bhkd->bhqk', qi, k).astype(jnp.float32) * scale
        s = s + (ci[..., :, None] - c[:, :, None, :])
        s = jnp.where(kpos[None, :] <= qpos[:, None], s, -jnp.inf)
        p = jax.nn.softmax(s, axis=-1).astype(v.dtype)
        return jnp.einsum('bhqk,bhkd->bhqd', p, v)

    o = lax.map(block, (qb, cb, jnp.arange(nb)))
    return o.transpose(1, 0, 3, 2, 4).reshape(B_, S, H * Dh)


def segsum(a):
    L = a.shape[-1]
    cs = jnp.cumsum(a, axis=-1)
    seg = cs[..., :, None] - cs[..., None, :]
    mask = jnp.tril(jnp.ones((L, L), dtype=bool))
    return jnp.where(mask, seg, -jnp.inf)


def ssd_scan(x, dt, A, Bm, Cm):
    B_, S = x.shape[:2]
    L, G, N, P = SSD_CHUNK, SSD_GROUPS, SSD_STATE, SSD_HEAD_DIM
    E = SSD_HEADS // G
    nc = S // L
    xd = (x * dt[..., None]).reshape(B_, nc, L, G, E, P)
    a = (dt * A).reshape(B_, nc, L, G, E).transpose(0, 3, 4, 1, 2)
    Bc = Bm.reshape(B_, nc, L, G, N)
    Cc = Cm.reshape(B_, nc, L, G, N)
    a_cs = jnp.cumsum(a, axis=-1)
    Lmat = jnp.exp(segsum(a))
    cb = jnp.einsum('bclgn,bcsgn->bgcls', Cc, Bc)
    y_diag = jnp.einsum('bgcls,bgecls,bcsgep->bclgep', cb, Lmat, xd)
    decay = jnp.exp(a_cs[..., -1:] - a_cs)
    states = jnp.einsum('bclgn,bgecl,bclgep->cbgepn', Bc, decay, xd)
    chunk_decay = jnp.exp(a_cs[..., -1]).transpose(3, 0, 1, 2)

    def step(h, inp):
        s_c, d_c = inp
        return d_c[..., None, None] * h + s_c, h

    h0 = jnp.zeros(states.shape[1:], x.dtype)
    _, prev = lax.scan(step, h0, (states, chunk_decay))
    y_off = jnp.einsum('bclgn,cbgepn,bgecl->bclgep', Cc, prev, jnp.exp(a_cs))
    return (y_diag + y_off).reshape(B_, S, SSD_HEADS, P)


def ssd_mixer(z, xbc, dt_raw, conv_w, conv_b, dt_bias, a_log, d_skip, norm_w):
    B_, S, _ = z.shape
    f32 = jnp.float32
    xbc = jax.nn.silu(causal_dwconv(xbc, conv_w, conv_b))
    xs, Bm, Cm = jnp.split(xbc, [SSD_WIDTH, SSD_WIDTH + SSD_GROUPS * SSD_STATE], axis=-1)
    dt = jax.nn.softplus(dt_raw.astype(f32) + dt_bias.astype(f32))
    A = -jnp.exp(a_log.astype(f32))
    x = xs.astype(f32).reshape(B_, S, SSD_HEADS, SSD_HEAD_DIM)
    y = ssd_scan(x, dt, A,
                 Bm.astype(f32).reshape(B_, S, SSD_GROUPS, SSD_STATE),
                 Cm.astype(f32).reshape(B_, S, SSD_GROUPS, SSD_STATE))
    y = y + d_skip.astype(f32)[:, None] * x
    y = y.reshape(B_, S, SSD_WIDTH) * jax.nn.silu(z.astype(f32))
    yg = y.reshape(B_, S, SSD_GROUPS, SSD_WIDTH // SSD_GROUPS)
    yg = yg * lax.rsqrt(jnp.mean(yg * yg, axis=-1, keepdims=True) + NORM_EPS)
    return (yg.reshape(B_, S, SSD_WIDTH) * norm_w.astype(f32)).astype(z.dtype)


def conv_ffn(u, w_up, conv_w, conv_b, w_down):
    h = causal_dwconv(u @ w_up, conv_w, conv_b)
    g, val = jnp.split(h, 2, axis=-1)
    return (jax.nn.silu(g) * val) @ w_down


def setup_inputs(seed: int = 0) -> dict:
    key = jax.random.key(seed)
    ks = jax.random.split(key, 24)
    f32 = jnp.float32
    L = DEPTH

    def nrm(k, shape, scale):
        return jax.random.normal(k, shape, f32) * scale

    def gain(k, shape):
        return 1.0 + 0.02 * jax.random.normal(k, shape, f32)

    dt0 = jnp.exp(jax.random.uniform(ks[9], (L, SSD_HEADS), f32,
                                     math.log(1e-3), math.log(1e-1)))
    return {
        'x': jax.random.normal(ks[0], (BATCH, SEQ, D_MODEL), f32),
        'norm_mix': gain(ks[1], (L, D_MODEL)),
        'w_in': nrm(ks[2], (L, D_MODEL, IN_TOTAL), D_MODEL ** -0.5),
        'pool_mix': nrm(ks[3], (L, POOL_GROUPS, POOL_GROUP_DIM, POOL_GROUP_DIM), POOL_GROUP_DIM ** -0.5),
        'pool_scale': 1.0 + 0.1 * jax.random.normal(ks[4], (L, POOL_WIDTH), f32),
        'f_bias': jax.random.uniform(ks[5], (L, ATTN_HEADS), f32, 1.0, 4.0),
        'ssd_conv_w': nrm(ks[6], (L, SSD_CONV, SSD_CONV_CH), SSD_CONV ** -0.5),
        'ssd_conv_b': nrm(ks[7], (L, SSD_CONV_CH), 0.02),
        'ssd_dt_bias': dt0 + jnp.log(-jnp.expm1(-dt0)),
        'ssd_a_log': jnp.log(jax.random.uniform(ks[10], (L, SSD_HEADS), f32, 1.0, 16.0)),
        'ssd_d': 1.0 + 0.1 * jax.random.normal(ks[11], (L, SSD_HEADS), f32),
        'ssd_norm': gain(ks[12], (L, SSD_WIDTH)),
        'p_pool': nrm(ks[13], (L, POOL_WIDTH, D_MODEL), POOL_WIDTH ** -0.5),
        'p_attn': nrm(ks[14], (L, ATTN_WIDTH, D_MODEL), ATTN_WIDTH ** -0.5),
        'p_ssd': nrm(ks[15], (L, SSD_WIDTH, D_MODEL), SSD_WIDTH ** -0.5),
        'w_out': nrm(ks[16], (L, D_MODEL, D_MODEL), D_MODEL ** -0.5),
        'norm_ffn': gain(ks[17], (L, D_MODEL)),
        'ffn_up': nrm(ks[18], (L, D_MODEL, 2 * FFN_DIM), D_MODEL ** -0.5),
        'ffn_conv_w': nrm(ks[19], (L, FFN_CONV, 2 * FFN_DIM), FFN_CONV ** -0.5),
        'ffn_conv_b': nrm(ks[20], (L, 2 * FFN_DIM), 0.02),
        'ffn_down': nrm(ks[21], (L, FFN_DIM, D_MODEL), FFN_DIM ** -0.5),
        'norm_final': gain(ks[22], (D_MODEL,)),
    }


def reference(x, norm_mix, w_in, pool_mix, pool_scale, f_bias, ssd_conv_w, ssd_conv_b,
              ssd_dt_bias, ssd_a_log, ssd_d, ssd_norm, p_pool, p_attn, p_ssd, w_out,
              norm_ffn, ffn_up, ffn_conv_w, ffn_conv_b, ffn_down, norm_final):
    B_, S, D = x.shape
    for l in range(DEPTH):
        u = rmsnorm(x, norm_mix[l])
        (pool_v, q, k, v, f_logit, z, xbc, dt_raw, gate_logits) = split_columns(u @ w_in[l])
        y_pool = pool_mixer(pool_v, pool_mix[l], pool_scale[l]) @ p_pool[l]
        y_attn = forgetting_attention(q, k, v, f_logit, f_bias[l]) @ p_attn[l]
        y_ssd = ssd_mixer(z, xbc, dt_raw, ssd_conv_w[l], ssd_conv_b[l], ssd_dt_bias[l],
                          ssd_a_log[l], ssd_d[l], ssd_norm[l]) @ p_ssd[l]
        gates = jax.nn.sigmoid(gate_logits.astype(jnp.float32)).astype(x.dtype)
        gates = gates.reshape(B_, S, N_BRANCH, D)
        merged = gates[:, :, 0] * y_pool + gates[:, :, 1] * y_attn + gates[:, :, 2] * y_ssd
        x = x + merged @ w_out[l]
        x = x + conv_ffn(rmsnorm(x, norm_ffn[l]), ffn_up[l], ffn_conv_w[l], ffn_conv_b[l], ffn_down[l])
    return rmsnorm(x, norm_final)
```

```python
import numpy as np
from contextlib import ExitStack
import concourse.bass as bass
import concourse.mybir as mybir
from concourse.bass_utils import run_bass_kernel_spmd

F32 = mybir.dt.float32
BF16 = mybir.dt.bfloat16
AF = mybir.ActivationFunctionType
ALU = mybir.AluOpType

D = 1024
T = 512
NB = 4
DEPTH = 2
IN_TOTAL = 7704
FFN = 2816
EPS = 1e-6


TAG = ['init']


class Prog:
    ENG = ['pe', 'act', 'dve', 'pool', 'sp']

    def __init__(self, nc, nds=24):
        self.nc = nc
        self.nds = nds
        self.nl = 5 + nds + 1
        self.CL = 5 + nds
        self.lane = {e: i for i, e in enumerate(self.ENG)}
        self.stream = {e: [] for e in self.ENG}
        self.cnt = [0] * self.nl
        self.known = {e: np.zeros(self.nl, np.int64) for e in self.ENG}
        self.snap = [dict() for _ in range(self.nl)]
        self.res = {}
        self.signal = [set() for _ in range(5)]
        self.ndma = 0
        self.nwaits = 0
        self.tags = {e: [] for e in self.ENG}

    def _deps(self, e, reads, writes):
        le = self.lane[e]
        ev = {}
        for r in reads:
            st = self.res.get(r)
            if st is not None and st[0] is not None:
                l, c = st[0]
                if c > ev.get(l, 0):
                    ev[l] = c
        for w in writes:
            st = self.res.get(w)
            if st is not None:
                if st[0] is not None:
                    l, c = st[0]
                    if c > ev.get(l, 0):
                        ev[l] = c
                for l, c in st[1].items():
                    if l == le:
                        continue
                    if c > ev.get(l, 0):
                        ev[l] = c
        return ev

    def _acquire(self, e, ev):
        k = self.known[e]
        le = self.lane[e]
        waits = []
        for l, c in sorted(ev.items(), key=lambda t: -t[1]):
            if k[l] >= c:
                continue
            if l == le and e == 'pe':
                continue
            waits.append((l, c))
            if l < 5:
                self.signal[l].add(c)
            np.maximum(k, self.snap[l][c], out=k)
        self.nwaits += len(waits)
        return waits

    def _mark(self, evt, reads, writes):
        l, c = evt
        for r in reads:
            st = self.res.get(r)
            if st is None:
                st = self.res[r] = [None, {}]
            if c > st[1].get(l, 0):
                st[1][l] = c
        for w in writes:
            self.res[w] = [evt, {}]

    def op(self, e, fn, reads=(), writes=()):
        waits = self._acquire(e, self._deps(e, reads, writes))
        le = self.lane[e]
        self.cnt[le] += 1
        c = self.cnt[le]
        s = self.known[e].copy()
        s[le] = c
        self.snap[le][c] = s
        self.stream[e].append(('op', fn, waits, c))
        self.tags[e].append(TAG[0])
        self._mark((le, c), reads, writes)

    def dma(self, q, out, in_, reads=(), writes=(), **kw):
        n = self.ndma
        self.ndma += 1
        l = 5 + n % self.nds
        ev = self._deps(q, reads, writes)
        if self.cnt[l] > 0:
            ev[l] = max(ev.get(l, 0), self.cnt[l])
        waits = self._acquire(q, ev)
        self.cnt[l] += 1
        c = self.cnt[l]
        s = self.known[q].copy()
        s[l] = c
        self.snap[l][c] = s
        self.stream[q].append(('dma', (out, in_, kw), waits, (l, c)))
        self._mark((l, c), reads, writes)

    def coll(self, cc, reads=(), writes=()):
        q = 'pool'
        l = self.CL
        ev = self._deps(q, reads, writes)
        if self.cnt[l] > 0:
            ev[l] = max(ev.get(l, 0), self.cnt[l])
        waits = self._acquire(q, ev)
        self.cnt[l] += 1
        c = self.cnt[l]
        s = self.known[q].copy()
        s[l] = c
        self.snap[l][c] = s
        self.stream[q].append(('coll', cc, waits, (l, c)))
        self._mark((l, c), reads, writes)

    def barrier(self):
        for e in self.ENG:
            ev = {}
            for l in range(self.nl):
                if self.cnt[l] > 0 and l != self.lane[e]:
                    ev[l] = self.cnt[l]
            waits = self._acquire(e, ev)
            self.stream[e].append(('wait', None, waits, 0))

    def finish(self, e='sp'):
        ev = {}
        for l in range(self.nl):
            if self.cnt[l] > 0 and l != self.lane[e]:
                ev[l] = self.cnt[l]
        waits = self._acquire(e, ev)
        self.stream[e].append(('wait', None, waits, 0))

    def emit(self):
        nc = self.nc
        with ExitStack() as st:
            sems = [st.enter_context(nc.semaphore(f"s{i}")) for i in range(self.nl)]
            rank = []
            for l in range(5):
                sig = sorted(self.signal[l])
                rank.append({c: i + 1 for i, c in enumerate(sig)})
            block = st.enter_context(nc.Block())

            def run(e, eng):
                le = self.lane[e]
                rk = rank[le]
                for kind, payload, waits, c in self.stream[e]:
                    for (l, cc) in waits:
                        v = rank[l][cc] if l < 5 else (cc if l == self.CL else 16 * cc)
                        eng.wait_ge(sems[l], v)
                    if kind == 'op':
                        ins = payload(eng)
                        if c in rk:
                            ins.then_inc(sems[le], 1)
                    elif kind == 'dma':
                        out, in_, kw = payload
                        eng.dma_start(out=out, in_=in_, **kw).then_inc(sems[c[0]], 16)
                    elif kind == 'coll':
                        cc = payload
                        eng.collective_compute(cc['kind'], cc['op'], replica_groups=cc['replica_groups'], ins=cc['ins'], outs=cc['outs']).then_inc(sems[self.CL], 1)

            @block.tensor
            def _(eng):
                run('pe', eng)

            @block.scalar
            def _(eng):
                run('act', eng)

            @block.vector
            def _(eng):
                run('dve', eng)

            @block.gpsimd
            def _(eng):
                run('pool', eng)

            @block.sync
            def _(eng):
                run('sp', eng)


class Rot:
    def __init__(self, items):
        self.items = items
        self.i = 0

    def next(self):
        it = self.items[self.i % len(self.items)]
        self.i += 1
        return it


WNAMES = ['x', 'norm_mix', 'w_in', 'pool_mix', 'pool_scale', 'f_bias', 'ssd_conv_w', 'ssd_conv_b',
          'ssd_dt_bias', 'ssd_a_log', 'ssd_d', 'ssd_norm', 'p_pool', 'p_attn', 'p_ssd', 'w_out',
          'norm_ffn', 'ffn_up', 'ffn_conv_w', 'ffn_conv_b', 'ffn_down', 'norm_final']
SHAPES = {
    'norm_mix': [2, 1024], 'w_in': [2, 1024, 7704], 'pool_mix': [2, 4, 128, 128], 'pool_scale': [2, 512],
    'f_bias': [2, 8], 'ssd_conv_w': [2, 4, 1536], 'ssd_conv_b': [2, 1536], 'ssd_dt_bias': [2, 16],
    'ssd_a_log': [2, 16], 'ssd_d': [2, 16], 'ssd_norm': [2, 1024], 'p_pool': [2, 512, 1024],
    'p_attn': [2, 512, 1024], 'p_ssd': [2, 1024, 1024], 'w_out': [2, 1024, 1024], 'norm_ffn': [2, 1024],
    'ffn_up': [2, 1024, 5632], 'ffn_conv_w': [2, 3, 5632], 'ffn_conv_b': [2, 5632],
    'ffn_down': [2, 2816, 1024], 'norm_final': [1024],
}


def block_defs(I, l):
    b = {}
    wi = I['w_in'][l]
    b['in_pool'] = (wi, 8, [(0, 512)])
    b['in_q'] = (wi, 8, [(512, 512)])
    b['in_k'] = (wi, 8, [(1024, 512)])
    b['in_v'] = (wi, 8, [(1536, 512)])
    b['in_f'] = (wi, 8, [(2048, 8)])
    b['in_z0'] = (wi, 8, [(2056, 512)])
    b['in_z1'] = (wi, 8, [(2568, 512)])
    for i in range(3):
        b[f'in_x{i}'] = (wi, 8, [(3080 + 512 * i, 512)])
    b['in_dt'] = (wi, 8, [(4616, 16)])
    for br in range(3):
        for dg in range(2):
            b[f'in_g{br}{dg}'] = (wi, 8, [(4632 + br * 1024 + dg * 512, 512)])
    b['mixw'] = (I['pool_mix'][l].rearrange("g c d -> (g c) d"), 4, [(0, 128)])
    for dg in range(2):
        b[f'p0{dg}'] = (I['p_pool'][l], 4, [(dg * 512, 512)])
        b[f'p1{dg}'] = (I['p_attn'][l], 4, [(dg * 512, 512)])
        b[f'p2{dg}'] = (I['p_ssd'][l], 8, [(dg * 512, 512)])
        b[f'wo{dg}'] = (I['w_out'][l], 8, [(dg * 512, 512)])
    for i in range(11):
        b[f'up{i}'] = (I['ffn_up'][l], 8, [(i * 256, 256), (FFN + i * 256, 256)])
    for d in range(4):
        b[f'dn0{d}'] = (I['ffn_down'][l][0:1536, :], 12, [(d * 256, 256)])
        b[f'dn1{d}'] = (I['ffn_down'][l][1536:2816, :], 10, [(d * 256, 256)])
    return b


def layer_plan(NSL):
    pl = ['in_pool', 'in_x0', 'in_x1', 'in_x2']
    for s in range(NSL):
        pl += ['in_k', 'in_v', 'in_f', 'in_x0', 'in_x1', 'in_x2', 'in_dt']
    for s in range(NSL):
        pl += ['in_pool', 'mixw', 'in_q', 'in_f', 'in_z0', 'in_z1']
        for dg in range(2):
            for br in range(3):
                pl += [f'in_g{br}{dg}', f'p{br}{dg}']
        pl += ['wo0', 'wo1']
    pl += [f'up{i}' for i in range(11)]
    for s in range(NSL):
        for half in range(2):
            rng = range(0, 6) if half == 0 else range(6, 11)
            pl += [f'up{i}' for i in rng]
            pl += [f'dn{half}{d}' for d in range(4)]
    return pl


def build(S, flags=('pool', 'attn', 'ssd', 'ffn'), dumps=()):
    NSL = S // T
    NS = NSL
    G = 3 * NSL
    NT = G + NSL + 1
    RF = 130
    RG = [[0, 1, 2, 3], [4, 5, 6, 7]]
    nc = bass.Bass("TRN2", target_bir_lowering=False)
    I = {}
    I['x'] = nc.dram_tensor("x", [S, D], F32, kind="ExternalInput").ap()
    for n in WNAMES[1:]:
        I[n] = nc.dram_tensor(n, SHAPES[n], F32, kind="ExternalInput").ap()
    I['corr'] = nc.dram_tensor("corr", [128, 64], F32, kind="ExternalInput").ap()
    I['xh0'] = nc.dram_tensor("xh0", [16, D], F32, kind="ExternalInput").ap()
    I['rk'] = nc.dram_tensor("rk", [128, 16], F32, kind="ExternalInput").ap()
    Y = nc.dram_tensor("y", [S, D], F32, kind="ExternalOutput").ap()
    DUMP = {}
    for (nm, shp) in dumps:
        DUMP[nm] = nc.dram_tensor("dbg_" + nm, list(shp), F32, kind="ExternalOutput").ap()

    bdefs = [block_defs(I, l) for l in range(DEPTH)]
    WS = {}
    for l in range(DEPTH):
        for nm, (src, kc, segs) in bdefs[l].items():
            ncols = sum(n for _, n in segs)
            WS[(l, nm)] = nc.dram_tensor(f"ws{l}_{nm}", [128, kc * ncols], BF16, kind="Internal").ap()
    XRES = [nc.dram_tensor(f"xres{s_}", [128, 8 * T], F32, kind="Internal").ap() for s_ in range(NSL)]
    KSs = [nc.dram_tensor(f"ks_src{i}", [8 * 70, T], BF16, kind="Internal").ap() for i in range(NSL)]
    VSs = [nc.dram_tensor(f"vs_src{i}", [8 * 128, T], BF16, kind="Internal").ap() for i in range(NSL)]
    KGs = [nc.dram_tensor(f"kg{i}", [4 * 8 * 70, T], BF16, kind="Internal").ap() for i in range(NSL)]
    VGs = [nc.dram_tensor(f"vg{i}", [4 * 8 * 128, T], BF16, kind="Internal").ap() for i in range(NSL)]

    def KSr(h, s_):
        return KSs[s_][h * 70:(h + 1) * 70, :]

    def VSr(h, s_):
        return VSs[s_][h * 128:(h + 1) * 128, :]
    FS = nc.dram_tensor("fs_src", [RF, 1024], F32, kind="Internal").ap()
    FG = nc.dram_tensor("fg", [4 * RF, 1024], F32, kind="Internal").ap()
    XSP = [nc.dram_tensor(f"xsp{i}", [128, NB * 1024], BF16, kind="Internal").ap() for i in range(NSL)]
    BSP = [nc.dram_tensor(f"bsp{i}", [128, NB * 256], BF16, kind="Internal").ap() for i in range(NSL)]
    CSP = [nc.dram_tensor(f"csp{i}", [128, 4 * T], BF16, kind="Internal").ap() for i in range(NSL)]
    DSP = [nc.dram_tensor(f"dsp{i}", [32, T], F32, kind="Internal").ap() for i in range(NSL)]
    HS = nc.dram_tensor("hs_src", [16, 1024], F32, kind="Internal").ap()
    HG = nc.dram_tensor("hg", [64, 1024], F32, kind="Internal").ap()

    st = ExitStack()
    with st:
        def sb(name, shape, dt=F32):
            return st.enter_context(nc.sbuf_tensor(name, shape, dt))

        def psum(name, shape, dt=F32):
            return st.enter_context(nc.psum_tensor(name, shape, dt))

        p = Prog(nc)

        def MM(out, lhsT, rhs, start, stop, rd, wr):
            p.op('pe', lambda e: e.matmul(out, lhsT=lhsT, rhs=rhs, start=start, stop=stop), rd, wr)

        def TR(out, in_, ident, rd, wr):
            p.op('pe', lambda e: e.transpose(out=out, in_=in_, identity=ident), rd, wr)

        def ACT(out, in_, func, rd, wr, bias=None, scale=None, accum=None):
            kw = {}
            if bias is not None:
                kw['bias'] = bias
            if scale is not None:
                kw['scale'] = scale
            if accum is not None:
                kw['accum_out'] = accum
            p.op('act', lambda e: e.activation(out=out, in_=in_, func=func, **kw), rd, wr)

        def TT(out, in0, in1, op, rd, wr, eng='dve'):
            p.op(eng, lambda e: e.tensor_tensor(out=out, in0=in0, in1=in1, op=op), rd, wr)

        def TS(out, in0, s1, s2, op0, op1, rd, wr, eng='dve'):
            if op1 is None:
                p.op(eng, lambda e: e.tensor_scalar(out=out, in0=in0, scalar1=s1, scalar2=None, op0=op0), rd, wr)
            else:
                p.op(eng, lambda e: e.tensor_scalar(out=out, in0=in0, scalar1=s1, scalar2=s2, op0=op0, op1=op1), rd, wr)

        def STT(out, in0, scalar, in1, op0, op1, rd, wr, eng='dve'):
            p.op(eng, lambda e: e.scalar_tensor_tensor(out=out, in0=in0, scalar=scalar, in1=in1, op0=op0, op1=op1), rd, wr)

        def CP(out, in_, rd, wr, eng='dve'):
            p.op(eng, lambda e: e.tensor_copy(out=out, in_=in_), rd, wr)

        def MS(ap, val, wr, eng='pool'):
            p.op(eng, lambda e: e.memset(ap, val), (), wr)

        def SCAN(out, d0, d1, init, rd, wr):
            p.op('dve', lambda e: e.tensor_tensor_scan(out=out, data0=d0, data1=d1, initial=init, op0=ALU.mult, op1=ALU.add), rd, wr)

        def dump(nm, ap_sb, rd, dst=None):
            if nm in DUMP:
                p.dma('sp', DUMP[nm] if dst is None else dst, ap_sb, reads=rd)

        ident_b = sb("ident_b", [128, 128], BF16)
        ident_f = sb("ident_f", [128, 128], F32)
        ones_b = sb("ones_b", [128, 128], BF16)
        ones_f = sb("ones_f", [128, 512], F32)
        zcol = sb("zcol", [128, 1], F32)
        onecol = sb("onecol", [128, 1], F32)
        epscol = sb("epscol", [128, 1], F32)
        maskT = sb("maskT", [128, 128], BF16)
        maskb = sb("maskb", [128, 128], F32)
        corr = sb("corr_sb", [128, 64], F32)
        rk = sb("rk_sb", [128, 16], F32)
        runacc = sb("runacc", [128, 16], F32)
        xhT = sb("xhT", [128, 8, 16], F32)
        uhT = sb("uhT", [128, 8, 16], BF16)
        goff = sb("goff", [8, 4, NSL + 1], F32)
        gtm = sb("gtm", [8, 4], F32)
        MS(ident_f[:], 1.0, ['ident_f'])
        p.op('pool', lambda e: e.affine_select(out=ident_f[:], in_=ident_f[:], pattern=[[-1, 128]], compare_op=ALU.is_equal, fill=0.0, base=0, channel_multiplier=1), ['ident_f'], ['ident_f'])
        CP(ident_b[:], ident_f[:], ['ident_f'], ['ident_b'])
        MS(ones_b[:], 1.0, ['ones_b'])
        MS(ones_f[:], 1.0, ['ones_f'])
        MS(zcol[:], 0.0, ['zcol'])
        MS(onecol[:], 1.0, ['onecol'])
        MS(epscol[:], EPS, ['epscol'])
        MS(maskT[:], 1.0, ['maskT'])
        p.op('pool', lambda e: e.affine_select(out=maskT[:], in_=maskT[:], pattern=[[1, 128]], compare_op=ALU.is_ge, fill=0.0, base=0, channel_multiplier=-1), ['maskT'], ['maskT'])
        MS(maskb[:], 0.0, ['maskb'])
        p.op('pool', lambda e: e.affine_select(out=maskb[:], in_=maskb[:], pattern=[[1, 128]], compare_op=ALU.is_ge, fill=-30000.0, base=0, channel_multiplier=-1), ['maskb'], ['maskb'])
        p.dma('sp', corr[:], I['corr'], writes=['corr'])
        p.dma('sp', rk[:], I['rk'], writes=['rk'])
        CONST = ['ident_f', 'ident_b', 'ones_b', 'ones_f', 'zcol', 'onecol', 'maskT', 'maskb', 'corr']

        psA = Rot([(psum(f"psA{i}", [128, 512]), f"psA{i}") for i in range(4)])
        psY = psum("psY", [128, 1024])
        psTb = psum("psTb", [128, 1024], BF16)
        psO = [(psum(f"psO{i}", [128, 512]), f"psO{i}") for i in range(1)]
        psX = Rot(psA.items + [psO[0], (psY[:, 0:512], 'psY'), (psY[:, 512:1024], 'psY2')])
        PSW = [psA]

        xT = sb("xT", [128, 8, T])
        uT = sb("uT", [128, 8, T], BF16)
        ypT = sb("ypT", [128, 4, T], BF16)
        atT = sb("atT", [128, 4, T], BF16)
        ysT = sb("ysT", [128, 8, T], BF16)
        NWS = 4
        wsl = [sb(f"wsl{i}", [128, 4096], BF16) for i in range(NWS)]
        Dsb = sb("Dsb", [128, 2048])
        tAB = sb("tAB", [128, 2048])
        P1 = [sb(f"P1_{l}", [128, 88]) for l in range(DEPTH)]
        P2 = [sb(f"P2_{l}", [128, 88]) for l in range(DEPTH)]
        P3 = [sb(f"P3_{l}", [128, 88]) for l in range(DEPTH)]
        PF = tAB[:, 1024:2048]
        negfb = sb("negfb", [8, DEPTH])
        dtb = sb("dtb", [16, DEPTH])
        negA = sb("negA", [16, DEPTH])
        Dvec = [sb(f"Dvec{l}", [128, 16]) for l in range(DEPTH)]
        pstg = sb("pstg", [128, 128])
        pcar = [sb(f"pcar{l}", [128, 4, 16]) for l in range(DEPTH)]
        scar = [sb(f"scar{l}", [128, 12, 3]) for l in range(DEPTH)]
        scarB = [sb(f"scarB{l}", [128, 12, 3]) for l in range(DEPTH)]
        fcar = [sb(f"fcar{l}", [128, 44, 2]) for l in range(DEPTH)]
        offtab = [sb(f"offtab{l}", [8, NT]) for l in range(DEPTH)]
        _sst = sb("Sst", [128, 1024])
        _sbf = sb("Sbf", [128, 1024], BF16)
        Sst = [_sst for l in range(DEPTH)]
        Sbf = [_sbf for l in range(DEPTH)]
        for l in range(DEPTH):
            MS(pcar[l][:], 0.0, [f'pcar{l}'])
            MS(scar[l][:], 0.0, [f'scar{l}'])
            MS(fcar[l][:], 0.0, [f'fcar{l}'])
            MS(offtab[l][:], 0.0, [f'offtab{l}'])
            MS(Sst[l][:], 0.0, ['Sst'])
            MS(Sbf[l][:], 0.0, ['Sbf'])

        def load_rows(rows, dst, dname):
            off = 0
            for ap, n in rows:
                p.dma('sp', pstg[off:off + n, :], ap, writes=['pstg'])
                off += n
            ps, pn = psA.next()
            TR(ps[:, 0:off], pstg[0:off, :], ident_f[0:off, 0:off], ['pstg', 'ident_f'], [pn])
            CP(dst[:, 0:off], ps[:, 0:off], [pn], [dname])

        def r128(ap1d):
            return ap1d.rearrange("(c p) -> c p", p=128)

        for l in range(DEPTH):
            load_rows([(r128(I['norm_mix'][l]), 8), (r128(I['norm_ffn'][l]), 8), (r128(I['ssd_norm'][l]), 8),
                       (r128(I['pool_scale'][l]), 4)] + [(r128(I['ssd_conv_w'][l][k]), 12) for k in range(4)]
                      + [(r128(I['ssd_conv_b'][l]), 12)], P1[l], f'P1_{l}')
            load_rows([(r128(I['ffn_conv_w'][l][0]), 44), (r128(I['ffn_conv_w'][l][1]), 44)], P2[l], f'P2_{l}')
            load_rows([(r128(I['ffn_conv_w'][l][2]), 44), (r128(I['ffn_conv_b'][l]), 44)], P3[l], f'P3_{l}')
            p.dma('sp', negfb[:, l:l + 1], I['f_bias'][l].rearrange("(h o) -> h o", o=1), writes=['negfb'])
            p.dma('sp', dtb[:, l:l + 1], I['ssd_dt_bias'][l].rearrange("(h o) -> h o", o=1), writes=['dtb'])
            p.dma('sp', negA[:, l:l + 1], I['ssd_a_log'][l].rearrange("(h o) -> h o", o=1), writes=['negA'])
            p.dma('sp', Dvec[l][:], I['ssd_d'][l].partition_broadcast(128), writes=[f'Dvec{l}'])
        TS(negfb[:], negfb[:], -1.0, None, ALU.mult, None, ['negfb'], ['negfb'])
        ACT(negA[:], negA[:], AF.Exp, ['negA'], ['negA'])
        TS(negA[:], negA[:], -1.0, None, ALU.mult, None, ['negA'], ['negA'])
        PAR = ['negfb', 'dtb', 'negA'] + [f'{a}_{l}' for a in ('P1', 'P2', 'P3') for l in range(DEPTH)] + [f'Dvec{l}' for l in range(DEPTH)]

        TAG[0] = 'prepass'
        WSRES = {}
        lp = layer_plan(NSL)
        with ExitStack() as pst:
            NST = 8
            pstA = [(pst.enter_context(nc.sbuf_tensor(f"pstA{i}", [128, 2048], F32)), f"pstA{i}") for i in range(NST)]
            pstB = [(pst.enter_context(nc.sbuf_tensor(f"pstB{i}", [128, 2048], BF16)), f"pstB{i}") for i in range(NST)]
            cast_eng = Rot(['dve', 'act', 'pool', 'dve', 'act'])
            units = []
            for l in range(DEPTH):
                seen = set()
                for nm in lp:
                    if nm in seen:
                        continue
                    seen.add(nm)
                    src, kc, segs = bdefs[l][nm]
                    ncols = sum(n_ for _, n_ in segs)
                    cstep = max(1, 2048 // ncols)
                    WSRES[(l, nm)] = []
                    for c0 in range(0, kc, cstep):
                        units.append((l, nm, c0, min(kc, c0 + cstep)))

            def pc_in(ui):
                l, nm, c0, c1 = units[ui]
                src, kc, segs = bdefs[l][nm]
                ncols = sum(n_ for _, n_ in segs)
                sg, sgn = pstA[ui % NST]
                nel = (c1 - c0) * ncols
                sv = sg[:, 0:nel].rearrange("p (c n) -> p c n", c=c1 - c0)
                off = 0
                for col, n_ in segs:
                    p.dma('sp', sv[:, :, off:off + n_], src[c0 * 128:c1 * 128, col:col + n_].rearrange("(c p) n -> p c n", p=128), writes=[sgn])
                    off += n_

            LAP = 7
            for ui in range(min(LAP, len(units))):
                pc_in(ui)
            for ui in range(len(units)):
                if ui + LAP < len(units):
                    pc_in(ui + LAP)
                l, nm, c0, c1 = units[ui]
                src, kc, segs = bdefs[l][nm]
                ncols = sum(n_ for _, n_ in segs)
                dstv = WS[(l, nm)].rearrange("p (c n) -> p c n", c=kc)
                sg, sgn = pstA[ui % NST]
                wv, wvn = pstB[ui % NST]
                nel = (c1 - c0) * ncols
                ce = cast_eng.next()
                if ce == 'act':
                    p.op('act', lambda e, o=wv[:, 0:nel], i=sg[:, 0:nel]: e.copy(out=o, in_=i), [sgn], [wvn])
                else:
                    CP(wv[:, 0:nel], sg[:, 0:nel], [sgn], [wvn], eng=ce)
                rn = ('ws', l, nm, c0)
                p.dma('sp', dstv[:, c0:c1, :], wv[:, 0:nel].rearrange("p (c n) -> p c n", c=c1 - c0), reads=[wvn], writes=[rn])
                WSRES[(l, nm)].append(rn)
            p.barrier()

        plan = [(l, nm) for l in range(DEPTH) for nm in lp]
        wstate = {'i': 0, 'issued': 0}

        def w_issue():
            k = wstate['issued']
            if k < len(plan):
                l, nm = plan[k]
                src, kc, segs = bdefs[l][nm]
                nel = kc * sum(n for _, n in segs)
                p.dma('sp', wsl[k % NWS][:, 0:nel], WS[(l, nm)], reads=WSRES[(l, nm)], writes=[f'w{k % NWS}'])
                wstate['issued'] += 1

        def w_get(l, nm):
            k = wstate['i']
            assert plan[k] == (l, nm), (plan[k], l, nm)
            src, kc, segs = bdefs[l][nm]
            ncols = sum(n for _, n in segs)
            return wsl[k % NWS][:, 0:kc * ncols].rearrange("p (c n) -> p c n", c=kc), f'w{k % NWS}'

        def w_rel():
            wstate['i'] += 1
            w_issue()

        for _ in range(NWS):
            w_issue()

        sq = Rot([(sb(f"sq{i}", [128, T], BF16), f"sq{i}") for i in range(2)])
        rstd = sb("rstd", [128, T])
        hb3 = Rot([(sb(f"hb{i}", [128, T + 16]), f"hb{i}") for i in range(3)])
        acc2 = Rot([(sb(f"acc{i}", [128, T]), f"acc{i}") for i in range(3)])
        sgt = [(sb(f"sg{i}", [128, T]), f"sg{i}") for i in range(4)]
        gt = Rot([(sb(f"gt{i}", [128, T]), f"gt{i}") for i in range(2)])
        xin = Rot([(Dsb[:, 0:1024], 'Dsb')])
        dT = Rot([(sb(f"dT{i}", [128, T], BF16), f"dT{i}") for i in range(2)])
        ssc = sb("ssc", [128, 4])
        qa = sb("qa", [128, 8, T], BF16)
        ka = sb("ka", [128, 8, T], BF16)
        vcur = sb("vcur", [128, NB, 8, 128], BF16)
        NKV = 3
        ksl = [sb(f"ksl{i}", [128, T], BF16) for i in range(NKV)]
        vsl = [sb(f"vsl{i}", [128, NB, 128], BF16) for i in range(NKV)]
        pTt = Rot([(sb(f"pT{i}", [128, T], BF16), f"pT{i}") for i in range(4)])
        rec = tAB[:, 1536:2048]
        fe = Dsb[0:8, 0:512]
        Cl = Dsb[0:8, 512:1024]
        r1 = Dsb[0:8, 1024:1536]
        hml = None
        dif8 = sb("dif8", [8, NT])
        BD = sb("BD", [8, 8, NT])
        biasb = sb("biasb", [128, 8, NT])
        MS(qa[64:70, :, :], 1.0, [('qaug', h_, r_) for h_ in range(8) for r_ in range(3)])
        MS(ka[64:70, :, :], 1.0, [('kaug', h_, r_) for h_ in range(8) for r_ in range(3)])
        MS(vcur[:], 1.0, [('v', h_) for h_ in range(8)])
        xc2 = Rot([(sb(f"xc{i}", [128, T], BF16), f"xc{i}") for i in range(2)])
        BCT = sb("BCT", [128, 4, T], BF16)
        xtok_t = sb("xtok", [128, NB * 1024], BF16)
        xtok = xtok_t[:].rearrange("p (b n) -> p b n", b=NB)
        mgT = xtok_t[:].rearrange("p (c n) -> p c n", c=8)
        Btok = sb("Btok", [128, NB, 256], BF16)
        big_t = sb("big", [128, 6144], BF16)
        zs = big_t[:, 0:4096].rearrange("p (b n) -> p b n", b=NB)
        EM = big_t[:, 4096:6144]
        actT = big_t[:].rearrange("p (c n) -> p c n", c=12)
        hml = big_t[0:8, 0:3072].rearrange("p (a r n) -> p a r n", a=2, r=3)
        xd = sb("xd", [128, 1024], BF16)
        xdd = sb("xdd", [128, 1024], BF16)
        ygn = sb("ygn", [128, 1024], BF16)
        CBs = sb("CBs", [128, 256])
        dtT = gt.items[0][0][0:16, :]
        aT = gt.items[1][0][0:16, :]
        acsT = sgt[0][0][0:16, :]
        Zbd = tAB[0:16, :].rearrange("p (h l) -> p h l", h=16)
        Zl = sb("Zl", [16, 16])
        tk = sb("tk", [128, 6, 16])

        def rmsnorm_to_uT(wcol_tile, wcol_name, wcol_off):
            ps, pn = psA.next()
            for c in range(8):
                s_, sn = sq.next()
                ACT(s_[:], xT[:, c, :], AF.Square, ['xT'], [sn])
                MM(ps[:], ones_b[:], s_[:], c == 0, c == 7, [sn, 'ones_b'], [pn])
            ACT(rstd[:], ps[:], AF.Sqrt, [pn, 'epscol'], ['rstd'], bias=epscol[:], scale=1.0 / D)
            p.op('dve', lambda e: e.reciprocal(out=rstd[:], in_=rstd[:]), ['rstd'], ['rstd'])
            for c in range(8):
                STT(uT[:, c, :], xT[:, c, :], wcol_tile[:, wcol_off + c:wcol_off + c + 1], rstd[:], ALU.mult, ALU.mult,
                    ['xT', 'rstd', wcol_name], ['uT'])

        def proj(w, wn, col0, ncols, src, srcn, kc):
            ps, pn = PSW[0].next()
            for c in range(kc):
                MM(ps[0:ncols, :], w[:, c, col0:col0 + ncols], src[:, c, :], c == 0, c == kc - 1, [wn, srcn], [pn])
            return ps, pn

        def conv_chunk(ps, pn, ntap, car, carn, ci, wt, wtn, wcols, bcol):
            H = ntap - 1
            hb, hn = hb3.next()
            a, an = acc2.next()
            carn = (carn, ci)
            hc, hm = (hn, 'c'), (hn, 'm')
            CP(hb[:, 0:H], car[:, ci, :], [carn], [hc], eng='pool')
            ACT(hb[:, H:H + T], ps[:], AF.Copy, [pn], [hm])
            ACT(a[:], ps[:], AF.Identity, [pn, wtn[-1]], [an], bias=bcol, scale=wcols[ntap - 1])
            CP(car[:, ci, :], hb[:, T:T + H], [hm], [carn], eng='pool')
            for k in range(ntap - 1):
                STT(a[:], hb[:, k:k + T], wcols[k], a[:], ALU.mult, ALU.add, [hc, hm, an] + wtn, [an])
            return a, an


        G = 3 * NSL

        def load_xT(s_):
            p.dma('sp', xT[:], XRES[s_].rearrange("p (c n) -> p c n", c=8), reads=[('xres', s_)], writes=['xT'])

        def store_xT(s_):
            p.dma('sp', XRES[s_].rearrange("p (c n) -> p c n", c=8), xT[:], reads=['xT'], writes=[('xres', s_)])

        def tok_major_block(b, dst, dstn):
            for cq in range(2):
                ps, pn = psA.next()
                for k in range(4):
                    c = cq * 4 + k
                    TR(ps[:, k * 128:(k + 1) * 128], xT[:, c, b * 128:(b + 1) * 128], ident_f[:], ['xT', 'ident_f'], [pn])
                ACT(dst[:, cq * 512:(cq + 1) * 512], ps[:], AF.Copy, [pn], [dstn])

        def halo_from(l, src_dram_or_none, gathered):
            ht = tAB[0:16, 1024:2048]
            if not gathered:
                p.dma('sp', ht, src_dram_or_none, writes=['tAB'])
            else:
                MS(ht, 0.0, ['tAB'], eng='dve')
                for q in range(4):
                    p.dma('sp', tAB[0:16, 0:1024], src_dram_or_none[q * 16:(q + 1) * 16, :], reads=['hg'], writes=['tAB'])
                    STT(ht, tAB[0:16, 0:1024], rk[0:16, 8 + q:9 + q], ht, ALU.mult, ALU.add, ['tAB', 'rk'], ['tAB'])
            ps, pn = psA.next()
            for c in range(8):
                TR(ps[:, c * 16:(c + 1) * 16], tAB[0:16, 1024 + c * 128:1024 + (c + 1) * 128], ident_f[0:16, 0:16], ['tAB', 'ident_f'], [pn])
            CP(xhT[:], ps[:, 0:128].rearrange("p (c n) -> p c n", c=8), [pn], ['xhT'])

        def norm_small(wcol_tile, wcol_name, wcol_off):
            ps, pn = psA.next()
            for c in range(8):
                s_, sn = sq.next()
                ACT(s_[:, 0:16], xhT[:, c, :], AF.Square, ['xhT'], [sn])
                MM(ps[:, 0:16], ones_b[:], s_[:, 0:16], c == 0, c == 7, [sn, 'ones_b'], [pn])
            ACT(rstd[:, 0:16], ps[:, 0:16], AF.Sqrt, [pn, 'epscol'], ['rstd'], bias=epscol[:], scale=1.0 / D)
            p.op('dve', lambda e: e.reciprocal(out=rstd[:, 0:16], in_=rstd[:, 0:16]), ['rstd'], ['rstd'])
            for c in range(8):
                STT(uhT[:, c, :], xhT[:, c, :], wcol_tile[:, wcol_off + c:wcol_off + c + 1], rstd[:, 0:16], ALU.mult, ALU.mult,
                    ['xhT', 'rstd', wcol_name], ['uhT'])

        def proj16(w, wn, col0, ncols):
            ps, pn = psA.next()
            for c in range(8):
                MM(ps[0:ncols, 0:16], w[:, c, col0:col0 + ncols], uhT[:, c, :], c == 0, c == 7, [wn, 'uhT'], [pn])
            return ps, pn

        def attn_f(l, s, phase):
            w, wn = w_get(l, 'in_f')
            ps, pn = psA.next()
            for c in range(8):
                MM(ps[0:8, :], w[:, c, 0:8], uT[:, c, :], c == 0, c == 7, [wn, 'uT'], [pn])
            w_rel()
            ACT(fe, ps[0:8, :], AF.Exp, [pn, 'negfb'], ['Dsb'], bias=negfb[:, l:l + 1], scale=-1.0)
            ACT(fe, fe, AF.Ln, ['Dsb', 'onecol'], ['Dsb'], bias=onecol[0:8, :], scale=1.0)
            SCAN(Cl, ones_f[0:8, 0:T], fe, 0.0, ['Dsb', 'ones_f'], ['Dsb'])
            if phase == 'A':
                TT(offtab[l][:, G + s + 1:G + s + 2], offtab[l][:, G + s:G + s + 1], Cl[:, T - 1:T], ALU.add, ['Dsb', f'offtab{l}'], [f'offtab{l}'])
            CP(hml[:, 1, 0, :], Cl, ['Dsb'], ['zs'])
            TT(r1, Cl, hml[:, 1, 0, :], ALU.subtract, ['Dsb', 'zs'], ['Dsb'])
            CP(hml[:, 1, 1, :], r1, ['Dsb'], ['zs'])
            TT(r1, r1, hml[:, 1, 1, :], ALU.subtract, ['Dsb', 'zs'], ['Dsb'])
            CP(hml[:, 1, 2, :], r1, ['Dsb'], ['zs'])
            if phase == 'B':
                TS(hml[:, 0, :, :], hml[:, 1, :, :], -1.0, None, ALU.mult, None, ['zs'], ['zs'])
            for h in range(8):
                for r in range(3):
                    if phase == 'B':
                        p.dma('sp', qa[64 + r:65 + r, h, :], hml[h:h + 1, 0, r, :], reads=['zs'], writes=[('qaug', h, r)])
                    else:
                        p.dma('sp', ka[67 + r:68 + r, h, :], hml[h:h + 1, 1, r, :], reads=['zs'], writes=[('kaug', h, r)])

        def ssd_prep(l, s, phase, car, carn):
            if phase == 'B':
                for zb in range(2):
                    w, wn = w_get(l, f'in_z{zb}')
                    for b in range(NB):
                        ps, pn = psA.next()
                        for c in range(8):
                            MM(ps[:], uT[:, c, b * 128:(b + 1) * 128], w[:, c, :], c == 0, c == 7, [wn, 'uT'], [pn])
                        ACT(zs[:, b, zb * 512:(zb + 1) * 512], ps[:], AF.Silu, [pn], ['zs'])
                    w_rel()
            if phase == 'B':
                p.dma('sp', xtok, XSP[s].rearrange("p (b n) -> p b n", b=NB), reads=[('xsp', s)], writes=['xtok'])
                p.dma('sp', Btok[:], BSP[s].rearrange("p (b n) -> p b n", b=NB), reads=[('bsp', s)], writes=['Btok'])
                p.dma('sp', BCT[:], CSP[s].rearrange("p (c n) -> p c n", c=4), reads=[('csp', s)], writes=['BCT'])
                p.dma('sp', dtT, DSP[s][0:16, :], reads=[('dsp0', s)], writes=['gt0'])
                p.dma('sp', acsT, DSP[s][16:32, :], reads=[('dsp1', s)], writes=['sg0'])
                return
            pend_tr = []
            pend_silu = []
            for xb in range(3):
                w, wn = w_get(l, f'in_x{xb}')
                for k4 in range(4):
                    ci = xb * 4 + k4
                    ps, pn = proj(w, wn, k4 * 128, 128, uT, 'uT', 8)
                    wc = [P1[l][:, 28 + k * 12 + ci:29 + k * 12 + ci] for k in range(4)]
                    a, an = conv_chunk(ps, pn, 4, car, carn, ci, P1[l], [f'P1_{l}'], wc, P1[l][:, 76 + ci:77 + ci])
                    while pend_silu:
                        pend_silu.pop(0)()
                    if pend_tr:
                        pend_tr.pop(0)()
                    if ci < 10:
                        if ci < 8:
                            xc, xcn = xc2.next()
                            pend_silu.append(lambda xc=xc, a=a, an=an, xcn=xcn: ACT(xc[:], a[:], AF.Silu, [an], [xcn]))
                            src_ap = xc
                        else:
                            xcn = 'BCT'
                            pend_silu.append(lambda ci=ci, a=a, an=an, xcn=xcn: ACT(BCT[:, ci - 8, :], a[:], AF.Silu, [an], [xcn]))
                            src_ap = BCT[:, ci - 8, :]

                        def do_tr(ci=ci, src_ap=src_ap, xcn=xcn):
                            po = (ci % 2) * 512
                            pbn = f'psTb{ci % 2}'
                            for b in range(NB):
                                TR(psTb[:, po + b * 128:po + (b + 1) * 128], src_ap[:, b * 128:(b + 1) * 128], ident_b[:], [xcn, 'ident_b'], [pbn])
                            if ci < 8:
                                p.op('act', lambda e, o_=xtok[:, :, ci * 128:(ci + 1) * 128], i_=psTb[:, po:po + 512].rearrange("p (b n) -> p b n", b=NB): e.copy(out=o_, in_=i_), [pbn], ['xtok'])
                            else:
                                p.op('act', lambda e, o_=Btok[:, :, (ci - 8) * 128:(ci - 7) * 128], i_=psTb[:, po:po + 512].rearrange("p (b n) -> p b n", b=NB): e.copy(out=o_, in_=i_), [pbn], ['Btok'])
                        pend_tr.append(do_tr)
                    else:
                        pend_silu.append(lambda ci=ci, a=a, an=an: ACT(BCT[:, ci - 8, :], a[:], AF.Silu, [an], ['BCT']))
                w_rel()
            while pend_silu:
                pend_silu.pop(0)()
            while pend_tr:
                pend_tr.pop(0)()
            w, wn = w_get(l, 'in_dt')
            ps, pn = psA.next()
            for c in range(8):
                MM(ps[0:16, :], w[:, c, 0:16], uT[:, c, :], c == 0, c == 7, [wn, 'uT'], [pn])
            w_rel()
            ACT(dtT, ps[0:16, :], AF.Exp, [pn, 'dtb'], ['gt0'], bias=dtb[:, l:l + 1], scale=1.0)
            ACT(dtT, dtT, AF.Ln, ['gt0', 'onecol'], ['gt0'], bias=onecol[0:16, :], scale=1.0)
            TS(aT, dtT, negA[:, l:l + 1], None, ALU.mult, None, ['gt0', 'negA'], ['gt1'])
            for b in range(NB):
                SCAN(acsT[:, b * 128:(b + 1) * 128], ones_f[0:16, 0:128], aT[:, b * 128:(b + 1) * 128], 0.0, ['gt1', 'ones_f'], ['sg0'])
            p.dma('sp', XSP[s].rearrange("p (b n) -> p b n", b=NB), xtok, reads=['xtok'], writes=[('xsp', s)])
            p.dma('sp', BSP[s].rearrange("p (b n) -> p b n", b=NB), Btok[:], reads=['Btok'], writes=[('bsp', s)])
            p.dma('sp', CSP[s].rearrange("p (c n) -> p c n", c=4), BCT[:], reads=['BCT'], writes=[('csp', s)])
            p.dma('sp', DSP[s][0:16, :], dtT, reads=['gt0'], writes=[('dsp0', s)])
            p.dma('sp', DSP[s][16:32, :], acsT, reads=['sg0'], writes=[('dsp1', s)])

        def ssd_tables(b):
            bs = slice(b * 128, (b + 1) * 128)
            ps, pn = psA.next()
            TR(ps[:, 0:16], acsT[:, bs], ident_f[0:16, 0:16], ['sg0', 'ident_f'], [pn])
            TR(ps[:, 16:32], dtT[:, bs], ident_f[0:16, 0:16], ['gt0', 'ident_f'], [pn])
            TS(Zl[:], ident_f[0:16, 0:16], acsT[:, b * 128 + 127:b * 128 + 128], None, ALU.mult, None, ['sg0', 'ident_f'], ['Zl'])
            MM(ps[:, 32:48], ones_f[0:16, 0:128], Zl[:], True, True, ['Zl', 'ones_f'], [pn])
            CP(tk[:, 0:3, :], ps[:, 0:48].rearrange("p (a h) -> p a h", a=3), [pn], ['tk'])
            ACT(tk[:, 3, :], tk[:, 2, :], AF.Exp, ['tk'], ['tk'])
            TT(tk[:, 4, :], tk[:, 2, :], tk[:, 0, :], ALU.subtract, ['tk'], ['tk'])
            ACT(tk[:, 4, :], tk[:, 4, :], AF.Exp, ['tk'], ['tk'])
            ACT(tk[:, 5, :], tk[:, 0, :], AF.Exp, ['tk'], ['tk'])

        def ssd_xd(b, eng2='dve'):
            TT(xd[:].rearrange("p (h d) -> p h d", h=16), xtok[:, b, :].rearrange("p (h d) -> p h d", h=16),
               tk[:, 1, :].unsqueeze(2).to_broadcast([128, 16, 64]), ALU.mult, ['xtok', 'tk'], ['xd'])
            TT(xdd[:].rearrange("p (h d) -> p h d", h=16), xd[:].rearrange("p (h d) -> p h d", h=16),
               tk[:, 4, :].unsqueeze(2).to_broadcast([128, 16, 64]), ALU.mult, ['xd', 'tk'], ['xdd'], eng=eng2)

        def ssd_state(l, b, with_bf):
            for g in range(2):
                ps, pn = psA.next()
                MM(ps[:], Btok[:, b, g * 128:(g + 1) * 128], xdd[:, g * 512:(g + 1) * 512], True, True, ['Btok', 'xdd'], [pn])
                TT(Sst[l][:, g * 512:(g + 1) * 512].rearrange("p (h d) -> p h d", h=8), Sst[l][:, g * 512:(g + 1) * 512].rearrange("p (h d) -> p h d", h=8),
                   tk[:, 3, g * 8:(g + 1) * 8].unsqueeze(2).to_broadcast([128, 8, 64]), ALU.mult, ['Sst', 'tk'], ['Sst'])
                TT(Sst[l][:, g * 512:(g + 1) * 512], Sst[l][:, g * 512:(g + 1) * 512], ps[:], ALU.add, ['Sst', pn], ['Sst'])
            if with_bf:
                p.op('act', lambda e, o_=Sbf[l][:], i_=Sst[l][:]: e.copy(out=o_, in_=i_), ['Sst'], ['Sbf'])

        TAG[0] = 'prepass'
        for s in range(NSL):
            for b in range(NB):
                xi, xn = xin.next()
                p.dma('sp', xi, I['x'][s * T + b * 128:s * T + (b + 1) * 128, :], writes=[xn])
                for cq in range(2):
                    ps, pn = psA.next()
                    for k in range(4):
                        c = cq * 4 + k
                        TR(ps[:, k * 128:(k + 1) * 128], xi[:, c * 128:(c + 1) * 128], ident_f[:], [xn, 'ident_f'], [pn])
                    p.op('act', lambda e, o=xT[:, cq * 4:cq * 4 + 4, b * 128:(b + 1) * 128], i=ps[:].rearrange("p (k n) -> p k n", k=4): e.copy(out=o, in_=i), [pn], ['xT'])
            store_xT(s)

        for l in range(DEPTH):
            TAG[0] = 'haloA'
            if l == 0:
                halo_from(l, I['xh0'], False)
            else:
                halo_from(l, HG, True)
            norm_small(P1[l], f'P1_{l}', 0)
            w, wn = w_get(l, 'in_pool')
            for g in range(4):
                ps, pn = proj16(w, wn, g * 128, 128)
                CP(pcar[l][:, g, :], ps[:, 0:16], [pn], [(f'pcar{l}', g)])
            w_rel()
            for xb in range(3):
                w, wn = w_get(l, f'in_x{xb}')
                for k4 in range(4):
                    ci = xb * 4 + k4
                    ps, pn = proj16(w, wn, k4 * 128, 128)
                    CP(scar[l][:, ci, :], ps[:, 13:16], [pn], [(f'scar{l}', ci)])
                    CP(scarB[l][:, ci, :], ps[:, 13:16], [pn], [f'scarB{l}'])
                w_rel()
            MS(Sst[l][:], 0.0, ['Sst'])
            MS(runacc[:], 0.0, ['runacc'], eng='dve')

            for s in range(NSL):
                TAG[0] = 'A_kv'
                PSW[0] = psX
                if s == 0:
                    load_xT(s)
                rmsnorm_to_uT(P1[l], f'P1_{l}', 0)
                load_xT(s + 1 if s + 1 < NSL else 0)
                w, wn = w_get(l, 'in_k')
                for j in range(4):
                    ps, pn = proj(w, wn, j * 128, 128, uT, 'uT', 8)
                    CP(ka[0:64, 2 * j, :], ps[0:64, :], [pn], [('ka', 2 * j)])
                    CP(ka[0:64, 2 * j + 1, :], ps[64:128, :], [pn], [('ka', 2 * j + 1)])
                w_rel()
                w, wn = w_get(l, 'in_v')
                for b in range(NB):
                    ps, pn = psA.next()
                    for c in range(8):
                        MM(ps[:], uT[:, c, b * 128:(b + 1) * 128], w[:, c, :], c == 0, c == 7, [wn, 'uT'], [pn])
                    p.op('act', lambda e, o=vcur[:, b, :, 0:64], i=ps[:].rearrange("p (h d) -> p h d", h=8): e.copy(out=o, in_=i), [pn], [('v', h_) for h_ in range(8)])
                w_rel()
                attn_f(l, s, 'A')
                for h in range(8):
                    p.dma('sp', KSr(h, s), ka[0:70, h, :], reads=[('ka', h)] + [('kaug', h, r_) for r_ in range(3)], writes=[('ks', h, s)])
                    p.dma('sp', VSr(h, s).rearrange("p (b n) -> p b n", b=NB), vcur[:, :, h, :], reads=[('v', h)], writes=[('vs', h, s)])
                TAG[0] = 'A_ssd'
                ssd_prep(l, s, 'A', scar[l], f'scar{l}')
                TAG[0] = 'A_chunk'
                for b in range(NB):
                    ssd_tables(b)
                    ssd_xd(b)
                    ssd_state(l, b, False)
                    TT(runacc[:], runacc[:], tk[:, 2, :], ALU.add, ['runacc', 'tk'], ['runacc'])
            PSW[0] = psA
            TAG[0] = 'exch'
            p.dma('sp', FS[0:128, :], Sst[l][:], reads=['Sst'], writes=['fs'])
            p.dma('sp', FS[128:129, 0:16], runacc[0:1, :], reads=['runacc'], writes=['fs'])
            p.dma('sp', FS[129:130, 0:8 * (NSL + 1)].rearrange("o (h j) -> (o h) j", h=8), offtab[l][:, G:G + NSL + 1], reads=[f'offtab{l}'], writes=['fs'])
            for s_ in range(NSL):
                ksr = [('ks', h, s_) for h in range(8)]
                vsr = [('vs', h, s_) for h in range(8)]
                p.coll(dict(kind="AllGather", op=ALU.bypass, replica_groups=RG, ins=[KSs[s_].opt()], outs=[KGs[s_].opt()]), reads=ksr, writes=[('kg', s_)])
                p.coll(dict(kind="AllGather", op=ALU.bypass, replica_groups=RG, ins=[VSs[s_].opt()], outs=[VGs[s_].opt()]), reads=vsr, writes=[('vg', s_)])
            p.coll(dict(kind="AllGather", op=ALU.bypass, replica_groups=RG, ins=[FS.opt()], outs=[FG.opt()]), reads=['fs'], writes=['fg'])

            MS(Sst[l][:], 0.0, ['Sst'])
            for q in range(3):
                p.dma('sp', tk[:, 0, :], FG[q * RF + 128, 0:16].partition_broadcast(128), reads=['fg'], writes=['tk'])
                ACT(tk[:, 1, :], tk[:, 0, :], AF.Exp, ['tk'], ['tk'])
                TS(tk[:, 2, :], tk[:, 1, :], rk[:, q:q + 1], rk[:, 12 + q:13 + q], ALU.mult, ALU.add, ['tk', 'rk'], ['tk'])
                p.dma('sp', Dsb[:, 0:1024], FG[q * RF:q * RF + 128, :], reads=['fg'], writes=['Dsb'])
                TT(Sst[l][:].rearrange("p (h d) -> p h d", h=16), Sst[l][:].rearrange("p (h d) -> p h d", h=16),
                   tk[:, 2, :].unsqueeze(2).to_broadcast([128, 16, 64]), ALU.mult, ['Sst', 'tk'], ['Sst'])
                STT(Sst[l][:], Dsb[:, 0:1024], rk[:, q:q + 1], Sst[l][:], ALU.mult, ALU.add, ['Dsb', 'rk', 'Sst'], ['Sst'])
            p.op('act', lambda e, o_=Sbf[l][:], i_=Sst[l][:]: e.copy(out=o_, in_=i_), ['Sst'], ['Sbf'])
            for q in range(4):
                p.dma('sp', goff[:, q, :], FG[q * RF + 129:q * RF + 130, 0:8 * (NSL + 1)].rearrange("o (h j) -> (o h) j", h=8), reads=['fg'], writes=['goff'])
            TT(gtm[:], goff[:, :, NSL], rk[0:8, 0:4], ALU.mult, ['goff', 'rk'], ['gtm'])
            for q in (2, 1, 0):
                TT(gtm[:, q:q + 1], gtm[:, q:q + 1], gtm[:, q + 1:q + 2], ALU.add, ['gtm'], ['gtm'])
            TT(gtm[:], rk[0:8, 4:8], gtm[:], ALU.subtract, ['gtm', 'rk'], ['gtm'])
            TT(offtab[l][:, 0:G].rearrange("p (q j) -> p q j", q=3), goff[:, 0:3, 0:NSL],
               gtm[:, 0:3].unsqueeze(2).to_broadcast([8, 3, NSL]), ALU.add, ['goff', 'gtm'], [f'offtab{l}'])

            for s in range(NSL):
                TAG[0] = 'B_pool'
                if s > 0:
                    load_xT(s)
                rmsnorm_to_uT(P1[l], f'P1_{l}', 0)
                w, wn = w_get(l, 'in_pool')
                wstate['i'] += 1
                mw, mwn = w_get(l, 'mixw')
                wstate['i'] -= 1
                pend_mix = []
                for g in range(4):
                    ps, pn = proj(w, wn, g * 128, 128, uT, 'uT', 8)
                    hb, hn = hb3.next()
                    hc, hm = (hn, 'c'), (hn, 'm')
                    CP(hb[:, 0:16], pcar[l][:, g, :], [(f'pcar{l}', g)], [hc])
                    ACT(hb[:, 16:16 + T], ps[:], AF.Copy, [pn], [hm])
                    CP(pcar[l][:, g, :], hb[:, T:T + 16], [hm], [(f'pcar{l}', g)])
                    cur, curn = hb, hn
                    pp = [(Dsb, 'Dsb'), (tAB, 'tAB')]
                    for j in range(g + 1):
                        sh = 1 << j
                        nx, nxn = pp[j % 2]
                        TT(nx[:, sh:T + 16], cur[:, sh:T + 16], cur[:, 0:T + 16 - sh], ALU.add, ([hc, hm] if j == 0 else [curn]), [nxn])
                        cur, curn = nx, nxn
                    wlen = 2 << g
                    if s == 0:
                        TT(cur[:, 16:32], cur[:, 16:32], corr[:, g * 16:(g + 1) * 16], ALU.mult, [curn, 'corr'], [curn])
                    d_, dn = dT.next()
                    STT(d_[:], cur[:, 16:16 + T], 1.0 / wlen, hb[:, 16:16 + T], ALU.mult, ALU.subtract, [curn, hm], [dn])
                    def do_mix(g=g, d_=d_, dn=dn):
                        psm, pmn = psA.next()
                        MM(psm[:], mw[:, g, :], d_[:], True, True, [mwn, dn], [pmn])
                        TS(ypT[:, g, :], psm[:], P1[l][:, 24 + g:25 + g], None, ALU.mult, None, [pmn, f'P1_{l}'], ['ypT'])
                    pend_mix.append(do_mix)
                    if len(pend_mix) > 1:
                        pend_mix.pop(0)()
                while pend_mix:
                    pend_mix.pop(0)()
                w_rel()
                w_rel()

                TAG[0] = 'B_attnprep'
                w, wn = w_get(l, 'in_q')
                for j in range(4):
                    ps, pn = proj(w, wn, j * 128, 128, uT, 'uT', 8)
                    ACT(qa[0:64, 2 * j, :], ps[0:64, :], AF.Identity, [pn, 'zcol'], [('qa', 2 * j)], bias=zcol[0:64, :], scale=0.125)
                    ACT(qa[0:64, 2 * j + 1, :], ps[64:128, :], AF.Identity, [pn, 'zcol'], [('qa', 2 * j + 1)], bias=zcol[0:64, :], scale=0.125)
                w_rel()
                attn_f(l, s, 'B')
                for h in range(8):
                    p.dma('sp', ka[0:70, h, :], KSr(h, s), reads=[('ks', h, s)], writes=[('ka', h)] + [('kaug', h, r_) for r_ in range(3)])
                    p.dma('sp', vcur[:, :, h, :], VSr(h, s).rearrange("p (b n) -> p b n", b=NB), reads=[('vs', h, s)], writes=[('v', h)])
                nu = G + s
                TS(dif8[:, 0:nu], offtab[l][:, 0:nu], offtab[l][:, nu:nu + 1], None, ALU.subtract, None, [f'offtab{l}'], ['dif8'])
                TT(BD[:, :, 0:nu], dif8[:, 0:nu].unsqueeze(1).to_broadcast([8, 8, nu]), ident_f[0:8, 0:8].unsqueeze(2).to_broadcast([8, 8, nu]), ALU.mult, ['dif8', 'ident_f'], ['BD'])
                ps, pn = psA.next()
                for h in range(8):
                    MM(ps[:, h * nu:(h + 1) * nu], ones_f[0:8, 0:128], BD[:, h, 0:nu], True, True, ['BD', 'ones_f'], [pn])
                CP(biasb[:, :, 0:nu], ps[:, 0:8 * nu].rearrange("p (h j) -> p h j", h=8), [pn], ['biasb'])
                units = [(h, u) for h in range(8) for u in range(nu)]
                ust = {'issued': 0}

                def kv_issue():
                    k = ust['issued']
                    if k < len(units):
                        h_, u_ = units[k]
                        if u_ < G:
                            q_, j_ = u_ // NSL, u_ % NSL
                            r0 = q_ * 8 + h_
                            ksrc, rk_ = KGs[j_][r0 * 70:(r0 + 1) * 70, :], [('kg', j_)]
                            vsrc, rv_ = VGs[j_][r0 * 128:(r0 + 1) * 128, :], [('vg', j_)]
                        else:
                            j_ = u_ - G
                            ksrc, rk_ = KSr(h_, j_), [('ks', h_, j_)]
                            vsrc, rv_ = VSr(h_, j_), [('vs', h_, j_)]
                        p.dma('sp', ksl[k % NKV][0:70, :], ksrc, reads=rk_, writes=[f'ksl{k % NKV}'])
                        p.dma('sp', vsl[k % NKV][:], vsrc.rearrange("p (b n) -> p b n", b=NB), reads=rv_, writes=[f'vsl{k % NKV}'])
                        ust['issued'] += 1

                for _ in range(NKV - 1):
                    kv_issue()
                ucount = 0
                TAG[0] = 'B_attn'
                flat = []
                for h in range(8):
                    groups = []
                    for u in range(nu):
                        kslot = ucount % NKV
                        for b in range(NB):
                            groups.append(('prev', u, b, kslot))
                        ucount += 1
                    for b in range(NB):
                        groups.append(('diag', s, b, None))
                    ng = len(groups)
                    for gi, (kind, j, b, kslot) in enumerate(groups):
                        flat.append((h, gi, ng, kind, j, b, kslot))
                obanks = [(psO[0][0], psO[0][1]), (psY[:, 0:512], 'psY')]
                LA = 2
                pend = []

                def emit_pv(item):
                    h, gi, ng, lv, rdv, pt, ptn, col0 = item
                    o, on = obanks[h % 2]
                    MM(o[:, col0:T], lv, pt[:, col0:T], gi == 0, gi == ng - 1, rdv + [ptn], [on])
                    if gi == ng - 1:
                        p.op('dve', lambda e, o_=rec[64:128, :], i_=o[64:128, :]: e.reciprocal(out=o_, in_=i_), [on], ['tAB'])
                        hp = (h % 2) * 64
                        TT(atT[hp:hp + 64, h // 2, :], o[0:64, :], rec[64:128, :], ALU.mult, [on, 'tAB'], ['atT'])

                for (h, gi, ng, kind, j, b, kslot) in flat:
                    if kind == 'prev':
                        lk = ksl[kslot][0:70, b * 128:(b + 1) * 128]
                        lv = vsl[kslot][:, b, :]
                        rdk, rdv = [f'ksl{kslot}'], [f'vsl{kslot}']
                        col0 = 0
                        bias_ap = biasb[:, h, j:j + 1]
                        rdb = ['biasb']
                    else:
                        lk = ka[0:70, h, b * 128:(b + 1) * 128]
                        lv = vcur[:, b, h, :]
                        rdk, rdv = [('ka', h)] + [('kaug', h, r_) for r_ in range(3)], [('v', h)]
                        col0 = b * 128
                        bias_ap = zcol[:]
                        rdb = ['zcol']
                    ps, pn = psA.next()
                    MM(ps[:, col0:T], lk, qa[0:70, h, col0:T], True, True, rdk + [('qa', h)] + [('qaug', h, r_) for r_ in range(3)], [pn])
                    pt, ptn = pTt.next()
                    ACT(pt[:, col0:T], ps[:, col0:T], AF.Exp, [pn] + rdb, [ptn], bias=bias_ap, scale=1.0)
                    if kind == 'diag':
                        TT(pt[:, col0:col0 + 128], pt[:, col0:col0 + 128], maskT[:], ALU.mult, [ptn, 'maskT'], [ptn])
                    pend.append((h, gi, ng, lv, rdv, pt, ptn, col0))
                    if kind == 'prev' and b == NB - 1:
                        kv_issue()
                    if len(pend) > LA:
                        emit_pv(pend.pop(0))
                while pend:
                    emit_pv(pend.pop(0))

                TAG[0] = 'B_ssdprep'
                ssd_prep(l, s, 'B', scarB[l], f'scarB{l}')
                TAG[0] = 'B_ssd'
                for b in range(NB):
                    bs = slice(b * 128, (b + 1) * 128)
                    ssd_tables(b)
                    TT(Zbd[:], acsT[:, bs].unsqueeze(1).to_broadcast([16, 16, 128]), ident_f[0:16, 0:16].unsqueeze(2).to_broadcast([16, 16, 128]), ALU.mult, ['sg0', 'ident_f'], ['tAB'])
                    for pc in range(4):
                        ps, pn = psA.next()
                        MM(ps[:], ones_f[0:16, 0:128], Zbd[:, pc * 4:(pc + 1) * 4, :].rearrange("k h l -> k (h l)"), True, True, ['tAB', 'ones_f'], [pn])
                        TT(Dsb[:, pc * 512:(pc + 1) * 512].rearrange("p (h l) -> p h l", h=4), ps[:].rearrange("p (h l) -> p h l", h=4),
                           tk[:, 0, pc * 4:(pc + 1) * 4].unsqueeze(2).to_broadcast([128, 4, 128]), ALU.subtract, [pn, 'tk'], ['Dsb'])
                    TT(Dsb[:].rearrange("p (h l) -> p h l", h=16), Dsb[:].rearrange("p (h l) -> p h l", h=16),
                       maskb[:].unsqueeze(1).to_broadcast([128, 16, 128]), ALU.add, ['Dsb', 'maskb'], ['Dsb'])
                    ACT(EM[:], Dsb[:], AF.Exp, ['Dsb'], ['EM'])
                    ps, pn = psA.next()
                    for g in range(2):
                        MM(ps[:, g * 128:(g + 1) * 128], BCT[:, g, bs], BCT[:, 2 + g, bs], True, True, ['BCT'], [pn])
                    CP(CBs[:], ps[:, 0:256], [pn], ['CBs'])
                    TT(EM[:].rearrange("p (g e l) -> p g e l", g=2, e=8), EM[:].rearrange("p (g e l) -> p g e l", g=2, e=8),
                       CBs[:].rearrange("p (g l) -> p g l", g=2).unsqueeze(2).to_broadcast([128, 2, 8, 128]), ALU.mult, ['EM', 'CBs'], ['EM'])
                    ssd_xd(b, 'pool')
                    for h in range(16):
                        MM(psY[:, h * 64:(h + 1) * 64], EM[:, h * 128:(h + 1) * 128], xd[:, h * 64:(h + 1) * 64], True, True, ['EM', 'xd'], ['psY', 'psY2'])
                    for g in range(2):
                        ps, pn = psA.next()
                        MM(ps[:], BCT[:, 2 + g, bs], Sbf[l][:, g * 512:(g + 1) * 512], True, True, ['BCT', 'Sbf'], [pn])
                        TT(tAB[:, g * 512:(g + 1) * 512].rearrange("p (h d) -> p h d", h=8), ps[:].rearrange("p (h d) -> p h d", h=8),
                           tk[:, 5, g * 8:(g + 1) * 8].unsqueeze(2).to_broadcast([128, 8, 64]), ALU.mult, [pn, 'tk'], ['tAB'])
                    TT(tAB[:, 0:1024], tAB[:, 0:1024], psY[:], ALU.add, ['tAB', 'psY', 'psY2'], ['tAB'])
                    TT(Dsb[:, 0:1024].rearrange("p (h d) -> p h d", h=16), xtok[:, b, :].rearrange("p (h d) -> p h d", h=16),
                       Dvec[l][:].unsqueeze(2).to_broadcast([128, 16, 64]), ALU.mult, ['xtok', f'Dvec{l}'], ['Dsb'], eng='pool')
                    TT(tAB[:, 0:1024], tAB[:, 0:1024], Dsb[:, 0:1024], ALU.add, ['tAB', 'Dsb'], ['tAB'])
                    TT(tAB[:, 0:1024], tAB[:, 0:1024], zs[:, b, :], ALU.mult, ['tAB', 'zs'], ['tAB'])
                    MS(ssc[:, 0:2], 0.0, ['ssc'], eng='dve')
                    for g in range(2):
                        ACT(tAB[:, 1024 + g * 512:1536 + g * 512], tAB[:, g * 512:(g + 1) * 512], AF.Square, ['tAB', 'ssc'], ['tAB', 'ssc'], accum=ssc[:, g:g + 1])
                    ACT(ssc[:, 2:4], ssc[:, 0:2], AF.Sqrt, ['ssc', 'epscol'], ['ssc'], bias=epscol[:], scale=1.0 / 512)
                    p.op('dve', lambda e: e.reciprocal(out=ssc[:, 2:4], in_=ssc[:, 2:4]), ['ssc'], ['ssc'])
                    TT(ygn[:].rearrange("p (g n) -> p g n", g=2), tAB[:, 0:1024].rearrange("p (g n) -> p g n", g=2),
                       ssc[:, 2:4].unsqueeze(2).to_broadcast([128, 2, 512]), ALU.mult, ['tAB', 'ssc'], ['ygn'])
                    for c in range(8):
                        TR(psTb[:, c * 128:(c + 1) * 128], ygn[:, c * 128:(c + 1) * 128], ident_b[:], ['ygn', 'ident_b'], ['psTb0', 'psTb1'])
                    TT(ysT[:, :, bs], psTb[:].rearrange("p (c n) -> p c n", c=8),
                       P1[l][:, 16:24].unsqueeze(2).to_broadcast([128, 8, 128]), ALU.mult, ['psTb0', 'psTb1', f'P1_{l}'], ['ysT'])
                    ssd_state(l, b, True)

                TAG[0] = 'B_merge'
                PSW[0] = psX
                brT = [(ypT, 'ypT', 4), (atT, 'atT', 4), (ysT, 'ysT', 8)]
                for dg in range(2):
                    for br in range(3):
                        wg, wgn = w_get(l, f'in_g{br}{dg}')
                        wstate['i'] += 1
                        wp, wpn = w_get(l, f'p{br}{dg}')
                        wstate['i'] -= 1
                        src, srcn, kc = brT[br]
                        for d4 in range(4):
                            ps, pn = proj(wg, wgn, d4 * 128, 128, uT, 'uT', 8)
                            g_, gn = gt.next()
                            ACT(g_[:], ps[:], AF.Sigmoid, [pn], [gn])
                            ps2, pn2 = proj(wp, wpn, d4 * 128, 128, src, srcn, kc)
                            acc_ap = Dsb[:, d4 * 512:(d4 + 1) * 512]
                            if br == 0:
                                TT(acc_ap, g_[:], ps2[:], ALU.mult, [gn, pn2], ['Dsb'])
                            else:
                                TT(g_[:], g_[:], ps2[:], ALU.mult, [gn, pn2], [gn])
                                if br == 1:
                                    TT(acc_ap, acc_ap, g_[:], ALU.add, ['Dsb', gn], ['Dsb'])
                                else:
                                    TT(mgT[:, dg * 4 + d4, :], acc_ap, g_[:], ALU.add, ['Dsb', gn], ['xtok'])
                        w_rel()
                        w_rel()
                for blk in range(2):
                    w, wn = w_get(l, f'wo{blk}')
                    for d4 in range(4):
                        ps, pn = proj(w, wn, d4 * 128, 128, mgT, 'xtok', 8)
                        TT(xT[:, blk * 4 + d4, :], xT[:, blk * 4 + d4, :], ps[:], ALU.add, ['xT', pn], ['xT'])
                    w_rel()
                PSW[0] = psA
                store_xT(s)
                if s == NSL - 1:
                    xi, xn = xin.next()
                    tok_major_block(NB - 1, xi, xn)
                    p.dma('sp', HS, xi[112:128, :], reads=[xn], writes=['hs'])
                    load_xT(0)
            p.coll(dict(kind="AllGather", op=ALU.bypass, replica_groups=RG, ins=[HS.opt()], outs=[HG.opt()]), reads=['hs'], writes=['hg'])

            TAG[0] = 'haloC'
            halo_from(l, HG, True)
            norm_small(P1[l], f'P1_{l}', 8)
            for i in range(11):
                w, wn = w_get(l, f'up{i}')
                for r in range(4):
                    kk = 2 * i + (r % 2)
                    cidx = kk if r < 2 else 22 + kk
                    ps, pn = proj16(w, wn, r * 128, 128)
                    CP(fcar[l][:, cidx, :], ps[:, 14:16], [pn], [(f'fcar{l}', cidx)])
                w_rel()

            if l == DEPTH - 1:
                p.dma('sp', PF, I['norm_final'].partition_broadcast(128), writes=['tAB'])
            for s in range(NSL):
                TAG[0] = 'C_ffn'
                if s > 0:
                    load_xT(s)
                rmsnorm_to_uT(P1[l], f'P1_{l}', 8)
                for half in range(2):
                    blks = range(0, 6) if half == 0 else range(6, 11)
                    nk = 12 if half == 0 else 10
                    TAG[0] = 'C_up'
                    PSW[0] = psX
                    pend_silu = []
                    for i in blks:
                        w, wn = w_get(l, f'up{i}')
                        for r in range(4):
                            kk = 2 * i + (r % 2)
                            cidx = kk if r < 2 else 22 + kk
                            ps, pn = proj(w, wn, r * 128, 128, uT, 'uT', 8)
                            wc = [P2[l][:, cidx:cidx + 1], P2[l][:, 44 + cidx:45 + cidx], P3[l][:, cidx:cidx + 1]]
                            a, an = conv_chunk(ps, pn, 3, fcar[l], f'fcar{l}', cidx, None, [f'P2_{l}', f'P3_{l}'], wc, P3[l][:, 44 + cidx:45 + cidx])
                            sgo = 2 * (i % 2)
                            while pend_silu:
                                pend_silu.pop(0)()
                            if r < 2:
                                def do_silu(o_=sgt[sgo + r][0], on_=sgt[sgo + r][1], a=a, an=an):
                                    ACT(o_[:], a[:], AF.Silu, [an], [on_])
                                pend_silu.append(do_silu)
                            else:
                                TT(actT[:, kk - (0 if half == 0 else 12), :], sgt[sgo + r - 2][0][:], a[:], ALU.mult, [sgt[sgo + r - 2][1], an], ['zs', 'EM'], eng='pool')
                        w_rel()
                    TAG[0] = 'C_dn'
                    for dd in range(4):
                        w, wn = w_get(l, f'dn{half}{dd}')
                        for d2 in range(2):
                            d = 2 * dd + d2
                            ps, pn = psA.next()
                            for k in range(nk):
                                MM(ps[:], w[:, k, d2 * 128:(d2 + 1) * 128], actT[:, k, :], k == 0, k == nk - 1, [wn, 'zs', 'EM'], [pn])
                            TT(xT[:, d, :], xT[:, d, :], ps[:], ALU.add, ['xT', pn], ['xT'])
                        w_rel()
                PSW[0] = psA
                if l < DEPTH - 1:
                    store_xT(s)
                    if s == NSL - 1:
                        xi, xn = xin.next()
                        tok_major_block(NB - 1, xi, xn)
                        p.dma('sp', HS, xi[112:128, :], reads=[xn], writes=['hs'])
                else:
                    for b in range(NB):
                        xi, xn = xin.next()
                        tok_major_block(b, xi, xn)
                        MS(ssc[:, 0:1], 0.0, ['ssc'], eng='dve')
                        ACT(tAB[:, 0:1024], xi, AF.Square, [xn, 'ssc'], ['tAB', 'ssc'], accum=ssc[:, 0:1])
                        ACT(ssc[:, 2:3], ssc[:, 0:1], AF.Sqrt, ['ssc', 'epscol'], ['ssc'], bias=epscol[:], scale=1.0 / D)
                        p.op('dve', lambda e: e.reciprocal(out=ssc[:, 2:3], in_=ssc[:, 2:3]), ['ssc'], ['ssc'])
                        STT(xi, xi, ssc[:, 2:3], PF, ALU.mult, ALU.mult, [xn, 'ssc', 'tAB'], [xn])
                        p.dma('sp', Y[s * T + b * 128:s * T + (b + 1) * 128, :], xi, reads=[xn])
            if l < DEPTH - 1:
                p.coll(dict(kind="AllGather", op=ALU.bypass, replica_groups=RG, ins=[HS.opt()], outs=[HG.opt()]), reads=['hs'], writes=['hg'])
        p.finish('sp')
        p.emit()
    return nc, p


_CACHE = {}


def make_corr():
    corr = np.ones((128, 64), np.float32)
    for g in range(4):
        w = 2 << g
        for t in range(16):
            corr[:, g * 16 + t] = w / min(t + 1, w)
    return corr


def kernel(**inputs):
    x = np.asarray(inputs['x'], np.float32)
    B, S, _ = x.shape
    SC = S // 4
    if SC not in _CACHE:
        _CACHE[SC] = build(SC)
    nc, _ = _CACHE[SC]
    n = 8
    in_maps = []
    for i in range(n):
        b, r = i // 4, i % 4
        rk = np.zeros((128, 16), np.float32)
        for q in range(4):
            rk[:, q] = 1.0 if q < r else 0.0
            rk[:, 4 + q] = 0.0 if q < r else -30000.0
            rk[:, 8 + q] = 1.0 if q == r - 1 else 0.0
            rk[:, 12 + q] = 0.0 if q < r else 1.0
        xh0 = np.zeros((16, D), np.float32) if r == 0 else np.ascontiguousarray(x[b, r * SC - 16:r * SC])
        m = {'x': np.ascontiguousarray(x[b, r * SC:(r + 1) * SC]), 'xh0': xh0, 'rk': rk,
             'corr': make_corr() if r == 0 else np.ones((128, 64), np.float32)}
        for k in WNAMES[1:]:
            m[k] = np.ascontiguousarray(np.asarray(inputs[k], np.float32))
        in_maps.append(m)
    res = run_bass_kernel_spmd(nc, in_maps, core_ids=list(range(n)))
    out = np.stack([np.concatenate([res.results[b * 4 + r]['y'] for r in range(4)], axis=0) for b in range(B)], axis=0)
    return out.astype(np.float32)
```

```python
import numpy as np
from contextlib import ExitStack
import concourse.bass as bass
import concourse.mybir as mybir
from concourse.bass_utils import run_bass_kernel_spmd

F32 = mybir.dt.float32
BF16 = mybir.dt.bfloat16
AF = mybir.ActivationFunctionType
ALU = mybir.AluOpType

D = 1024
T = 512
NB = 4
DEPTH = 2
IN_TOTAL = 7704
FFN = 2816
EPS = 1e-6


TAG = ['init']


class Prog:
    ENG = ['pe', 'act', 'dve', 'pool', 'sp']

    def __init__(self, nc, nds=24):
        self.nc = nc
        self.nds = nds
        self.nl = 5 + nds + 1
        self.CL = 5 + nds
        self.lane = {e: i for i, e in enumerate(self.ENG)}
        self.stream = {e: [] for e in self.ENG}
        self.cnt = [0] * self.nl
        self.known = {e: np.zeros(self.nl, np.int64) for e in self.ENG}
        self.snap = [dict() for _ in range(self.nl)]
        self.res = {}
        self.signal = [set() for _ in range(5)]
        self.ndma = 0
        self.nwaits = 0
        self.tags = {e: [] for e in self.ENG}

    def _deps(self, e, reads, writes):
        le = self.lane[e]
        ev = {}
        for r in reads:
            st = self.res.get(r)
            if st is not None and st[0] is not None:
                l, c = st[0]
                if c > ev.get(l, 0):
                    ev[l] = c
        for w in writes:
            st = self.res.get(w)
            if st is not None:
                if st[0] is not None:
                    l, c = st[0]
                    if c > ev.get(l, 0):
                        ev[l] = c
                for l, c in st[1].items():
                    if l == le:
                        continue
                    if c > ev.get(l, 0):
                        ev[l] = c
        return ev

    def _acquire(self, e, ev):
        k = self.known[e]
        le = self.lane[e]
        waits = []
        for l, c in sorted(ev.items(), key=lambda t: -t[1]):
            if k[l] >= c:
                continue
            if l == le and e == 'pe':
                continue
            waits.append((l, c))
            if l < 5:
                self.signal[l].add(c)
            np.maximum(k, self.snap[l][c], out=k)
        self.nwaits += len(waits)
        return waits

    def _mark(self, evt, reads, writes):
        l, c = evt
        for r in reads:
            st = self.res.get(r)
            if st is None:
                st = self.res[r] = [None, {}]
            if c > st[1].get(l, 0):
                st[1][l] = c
        for w in writes:
            self.res[w] = [evt, {}]

    def op(self, e, fn, reads=(), writes=()):
        waits = self._acquire(e, self._deps(e, reads, writes))
        le = self.lane[e]
        self.cnt[le] += 1
        c = self.cnt[le]
        s = self.known[e].copy()
        s[le] = c
        self.snap[le][c] = s
        self.stream[e].append(('op', fn, waits, c))
        self.tags[e].append(TAG[0])
        self._mark((le, c), reads, writes)

    def dma(self, q, out, in_, reads=(), writes=(), **kw):
        n = self.ndma
        self.ndma += 1
        l = 5 + n % self.nds
        ev = self._deps(q, reads, writes)
        if self.cnt[l] > 0:
            ev[l] = max(ev.get(l, 0), self.cnt[l])
        waits = self._acquire(q, ev)
        self.cnt[l] += 1
        c = self.cnt[l]
        s = self.known[q].copy()
        s[l] = c
        self.snap[l][c] = s
        self.stream[q].append(('dma', (out, in_, kw), waits, (l, c)))
        self._mark((l, c), reads, writes)

    def coll(self, cc, reads=(), writes=()):
        q = 'pool'
        l = self.CL
        ev = self._deps(q, reads, writes)
        if self.cnt[l] > 0:
            ev[l] = max(ev.get(l, 0), self.cnt[l])
        waits = self._acquire(q, ev)
        self.cnt[l] += 1
        c = self.cnt[l]
        s = self.known[q].copy()
        s[l] = c
        self.snap[l][c] = s
        self.stream[q].append(('coll', cc, waits, (l, c)))
        self._mark((l, c), reads, writes)

    def barrier(self):
        for e in self.ENG:
            ev = {}
            for l in range(self.nl):
                if self.cnt[l] > 0 and l != self.lane[e]:
                    ev[l] = self.cnt[l]
            waits = self._acquire(e, ev)
            self.stream[e].append(('wait', None, waits, 0))

    def finish(self, e='sp'):
        ev = {}
        for l in range(self.nl):
            if self.cnt[l] > 0 and l != self.lane[e]:
                ev[l] = self.cnt[l]
        waits = self._acquire(e, ev)
        self.stream[e].append(('wait', None, waits, 0))

    def emit(self):
        nc = self.nc
        with ExitStack() as st:
            sems = [st.enter_context(nc.semaphore(f"s{i}")) for i in range(self.nl)]
            rank = []
            for l in range(5):
                sig = sorted(self.signal[l])
                rank.append({c: i + 1 for i, c in enumerate(sig)})
            block = st.enter_context(nc.Block())

            def run(e, eng):
                le = self.lane[e]
                rk = rank[le]
                for kind, payload, waits, c in self.stream[e]:
                    for (l, cc) in waits:
                        v = rank[l][cc] if l < 5 else (cc if l == self.CL else 16 * cc)
                        eng.wait_ge(sems[l], v)
                    if kind == 'op':
                        ins = payload(eng)
                        if c in rk:
                            ins.then_inc(sems[le], 1)
                    elif kind == 'dma':
                        out, in_, kw = payload
                        eng.dma_start(out=out, in_=in_, **kw).then_inc(sems[c[0]], 16)
                    elif kind == 'coll':
                        cc = payload
                        eng.collective_compute(cc['kind'], cc['op'], replica_groups=cc['replica_groups'], ins=cc['ins'], outs=cc['outs']).then_inc(sems[self.CL], 1)

            @block.tensor
            def _(eng):
                run('pe', eng)

            @block.scalar
            def _(eng):
                run('act', eng)

            @block.vector
            def _(eng):
                run('dve', eng)

            @block.gpsimd
            def _(eng):
                run('pool', eng)

            @block.sync
            def _(eng):
                run('sp', eng)


class Rot:
    def __init__(self, items):
        self.items = items
        self.i = 0

    def next(self):
        it = self.items[self.i % len(self.items)]
        self.i += 1
        return it


WNAMES = ['x', 'norm_mix', 'w_in', 'pool_mix', 'pool_scale', 'f_bias', 'ssd_conv_w', 'ssd_conv_b',
          'ssd_dt_bias', 'ssd_a_log', 'ssd_d', 'ssd_norm', 'p_pool', 'p_attn', 'p_ssd', 'w_out',
          'norm_ffn', 'ffn_up', 'ffn_conv_w', 'ffn_conv_b', 'ffn_down', 'norm_final']
SHAPES = {
    'norm_mix': [2, 1024], 'w_in': [2, 1024, 7704], 'pool_mix': [2, 4, 128, 128], 'pool_scale': [2, 512],
    'f_bias': [2, 8], 'ssd_conv_w': [2, 4, 1536], 'ssd_conv_b': [2, 1536], 'ssd_dt_bias': [2, 16],
    'ssd_a_log': [2, 16], 'ssd_d': [2, 16], 'ssd_norm': [2, 1024], 'p_pool': [2, 512, 1024],
    'p_attn': [2, 512, 1024], 'p_ssd': [2, 1024, 1024], 'w_out': [2, 1024, 1024], 'norm_ffn': [2, 1024],
    'ffn_up': [2, 1024, 5632], 'ffn_conv_w': [2, 3, 5632], 'ffn_conv_b': [2, 5632],
    'ffn_down': [2, 2816, 1024], 'norm_final': [1024],
}


def block_defs(I, l):
    b = {}
    wi = I['w_in'][l]
    b['in_pool'] = (wi, 8, [(0, 512)])
    b['in_q'] = (wi, 8, [(512, 512)])
    b['in_k'] = (wi, 8, [(1024, 512)])
    b['in_v'] = (wi, 8, [(1536, 512)])
    b['in_f'] = (wi, 8, [(2048, 8)])
    b['in_z0'] = (wi, 8, [(2056, 512)])
    b['in_z1'] = (wi, 8, [(2568, 512)])
    for i in range(3):
        b[f'in_x{i}'] = (wi, 8, [(3080 + 512 * i, 512)])
    b['in_dt'] = (wi, 8, [(4616, 16)])
    for br in range(3):
        for dg in range(2):
            b[f'in_g{br}{dg}'] = (wi, 8, [(4632 + br * 1024 + dg * 512, 512)])
    b['mixw'] = (I['pool_mix'][l].rearrange("g c d -> (g c) d"), 4, [(0, 128)])
    for dg in range(2):
        b[f'p0{dg}'] = (I['p_pool'][l], 4, [(dg * 512, 512)])
        b[f'p1{dg}'] = (I['p_attn'][l], 4, [(dg * 512, 512)])
        b[f'p2{dg}'] = (I['p_ssd'][l], 8, [(dg * 512, 512)])
        b[f'wo{dg}'] = (I['w_out'][l], 8, [(dg * 512, 512)])
    for i in range(11):
        b[f'up{i}'] = (I['ffn_up'][l], 8, [(i * 256, 256), (FFN + i * 256, 256)])
    for d in range(4):
        b[f'dn0{d}'] = (I['ffn_down'][l][0:1536, :], 12, [(d * 256, 256)])
        b[f'dn1{d}'] = (I['ffn_down'][l][1536:2816, :], 10, [(d * 256, 256)])
    return b


def layer_plan(NSL):
    pl = ['in_pool', 'in_x0', 'in_x1', 'in_x2']
    for s in range(NSL):
        pl += ['in_k', 'in_v', 'in_f', 'in_x0', 'in_x1', 'in_x2', 'in_dt']
    for s in range(NSL):
        pl += ['in_pool', 'mixw', 'in_q', 'in_f', 'in_z0', 'in_z1']
        for dg in range(2):
            for br in range(3):
                pl += [f'in_g{br}{dg}', f'p{br}{dg}']
        pl += ['wo0', 'wo1']
    pl += [f'up{i}' for i in range(11)]
    for s in range(NSL):
        for half in range(2):
            rng = range(0, 6) if half == 0 else range(6, 11)
            pl += [f'up{i}' for i in rng]
            pl += [f'dn{half}{d}' for d in range(4)]
    return pl


def build(S, flags=('pool', 'attn', 'ssd', 'ffn'), dumps=()):
    NSL = S // T
    NS = NSL
    G = 3 * NSL
    NT = G + NSL + 1
    RF = 130
    RG = [[0, 1, 2, 3], [4, 5, 6, 7]]
    nc = bass.Bass("TRN2", target_bir_lowering=False)
    I = {}
    I['x'] = nc.dram_tensor("x", [S, D], F32, kind="ExternalInput").ap()
    for n in WNAMES[1:]:
        I[n] = nc.dram_tensor(n, SHAPES[n], F32, kind="ExternalInput").ap()
    I['corr'] = nc.dram_tensor("corr", [128, 64], F32, kind="ExternalInput").ap()
    I['xh0'] = nc.dram_tensor("xh0", [16, D], F32, kind="ExternalInput").ap()
    I['rk'] = nc.dram_tensor("rk", [128, 16], F32, kind="ExternalInput").ap()
    Y = nc.dram_tensor("y", [S, D], F32, kind="ExternalOutput").ap()
    DUMP = {}
    for (nm, shp) in dumps:
        DUMP[nm] = nc.dram_tensor("dbg_" + nm, list(shp), F32, kind="ExternalOutput").ap()

    bdefs = [block_defs(I, l) for l in range(DEPTH)]
    WS = {}
    for l in range(DEPTH):
        for nm, (src, kc, segs) in bdefs[l].items():
            ncols = sum(n for _, n in segs)
            WS[(l, nm)] = nc.dram_tensor(f"ws{l}_{nm}", [128, kc * ncols], BF16, kind="Internal").ap()
    XRES = [nc.dram_tensor(f"xres{s_}", [128, 8 * T], F32, kind="Internal").ap() for s_ in range(NSL)]
    KSs = [nc.dram_tensor(f"ks_src{i}", [8 * 70, T], BF16, kind="Internal").ap() for i in range(NSL)]
    VSs = [nc.dram_tensor(f"vs_src{i}", [8 * 128, T], BF16, kind="Internal").ap() for i in range(NSL)]
    KGs = [nc.dram_tensor(f"kg{i}", [4 * 8 * 70, T], BF16, kind="Internal").ap() for i in range(NSL)]
    VGs = [nc.dram_tensor(f"vg{i}", [4 * 8 * 128, T], BF16, kind="Internal").ap() for i in range(NSL)]

    def KSr(h, s_):
        return KSs[s_][h * 70:(h + 1) * 70, :]

    def VSr(h, s_):
        return VSs[s_][h * 128:(h + 1) * 128, :]
    FS = nc.dram_tensor("fs_src", [RF, 1024], F32, kind="Internal").ap()
    FG = nc.dram_tensor("fg", [4 * RF, 1024], F32, kind="Internal").ap()
    XSP = [nc.dram_tensor(f"xsp{i}", [128, NB * 1024], BF16, kind="Internal").ap() for i in range(NSL)]
    BSP = [nc.dram_tensor(f"bsp{i}", [128, NB * 256], BF16, kind="Internal").ap() for i in range(NSL)]
    CSP = [nc.dram_tensor(f"csp{i}", [128, 4 * T], BF16, kind="Internal").ap() for i in range(NSL)]
    DSP = [nc.dram_tensor(f"dsp{i}", [32, T], F32, kind="Internal").ap() for i in range(NSL)]
    HS = nc.dram_tensor("hs_src", [16, 1024], F32, kind="Internal").ap()
    HG = nc.dram_tensor("hg", [64, 1024], F32, kind="Internal").ap()

    st = ExitStack()
    with st:
        def sb(name, shape, dt=F32):
            return st.enter_context(nc.sbuf_tensor(name, shape, dt))

        def psum(name, shape, dt=F32):
            return st.enter_context(nc.psum_tensor(name, shape, dt))

        p = Prog(nc)

        def MM(out, lhsT, rhs, start, stop, rd, wr):
            p.op('pe', lambda e: e.matmul(out, lhsT=lhsT, rhs=rhs, start=start, stop=stop), rd, wr)

        def TR(out, in_, ident, rd, wr):
            p.op('pe', lambda e: e.transpose(out=out, in_=in_, identity=ident), rd, wr)

        def ACT(out, in_, func, rd, wr, bias=None, scale=None, accum=None):
            kw = {}
            if bias is not None:
                kw['bias'] = bias
            if scale is not None:
                kw['scale'] = scale
            if accum is not None:
                kw['accum_out'] = accum
            p.op('act', lambda e: e.activation(out=out, in_=in_, func=func, **kw), rd, wr)

        def TT(out, in0, in1, op, rd, wr, eng='dve'):
            p.op(eng, lambda e: e.tensor_tensor(out=out, in0=in0, in1=in1, op=op), rd, wr)

        def TS(out, in0, s1, s2, op0, op1, rd, wr, eng='dve'):
            if op1 is None:
                p.op(eng, lambda e: e.tensor_scalar(out=out, in0=in0, scalar1=s1, scalar2=None, op0=op0), rd, wr)
            else:
                p.op(eng, lambda e: e.tensor_scalar(out=out, in0=in0, scalar1=s1, scalar2=s2, op0=op0, op1=op1), rd, wr)

        def STT(out, in0, scalar, in1, op0, op1, rd, wr, eng='dve'):
            p.op(eng, lambda e: e.scalar_tensor_tensor(out=out, in0=in0, scalar=scalar, in1=in1, op0=op0, op1=op1), rd, wr)

        def CP(out, in_, rd, wr, eng='dve'):
            p.op(eng, lambda e: e.tensor_copy(out=out, in_=in_), rd, wr)

        def MS(ap, val, wr, eng='pool'):
            p.op(eng, lambda e: e.memset(ap, val), (), wr)

        def SCAN(out, d0, d1, init, rd, wr):
            p.op('dve', lambda e: e.tensor_tensor_scan(out=out, data0=d0, data1=d1, initial=init, op0=ALU.mult, op1=ALU.add), rd, wr)

        def dump(nm, ap_sb, rd, dst=None):
            if nm in DUMP:
                p.dma('sp', DUMP[nm] if dst is None else dst, ap_sb, reads=rd)

        ident_b = sb("ident_b", [128, 128], BF16)
        ident_f = sb("ident_f", [128, 128], F32)
        ones_b = sb("ones_b", [128, 128], BF16)
        ones_f = sb("ones_f", [128, 512], F32)
        zcol = sb("zcol", [128, 1], F32)
        onecol = sb("onecol", [128, 1], F32)
        epscol = sb("epscol", [128, 1], F32)
        maskT = sb("maskT", [128, 128], BF16)
        maskb = sb("maskb", [128, 128], F32)
        corr = sb("corr_sb", [128, 64], F32)
        rk = sb("rk_sb", [128, 16], F32)
        runacc = sb("runacc", [128, 16], F32)
        xhT = sb("xhT", [128, 8, 16], F32)
        uhT = sb("uhT", [128, 8, 16], BF16)
        goff = sb("goff", [8, 4, NSL + 1], F32)
        gtm = sb("gtm", [8, 4], F32)
        MS(ident_f[:], 1.0, ['ident_f'])
        p.op('pool', lambda e: e.affine_select(out=ident_f[:], in_=ident_f[:], pattern=[[-1, 128]], compare_op=ALU.is_equal, fill=0.0, base=0, channel_multiplier=1), ['ident_f'], ['ident_f'])
        CP(ident_b[:], ident_f[:], ['ident_f'], ['ident_b'])
        MS(ones_b[:], 1.0, ['ones_b'])
        MS(ones_f[:], 1.0, ['ones_f'])
        MS(zcol[:], 0.0, ['zcol'])
        MS(onecol[:], 1.0, ['onecol'])
        MS(epscol[:], EPS, ['epscol'])
        MS(maskT[:], 1.0, ['maskT'])
        p.op('pool', lambda e: e.affine_select(out=maskT[:], in_=maskT[:], pattern=[[1, 128]], compare_op=ALU.is_ge, fill=0.0, base=0, channel_multiplier=-1), ['maskT'], ['maskT'])
        MS(maskb[:], 0.0, ['maskb'])
        p.op('pool', lambda e: e.affine_select(out=maskb[:], in_=maskb[:], pattern=[[1, 128]], compare_op=ALU.is_ge, fill=-30000.0, base=0, channel_multiplier=-1), ['maskb'], ['maskb'])
        p.dma('sp', corr[:], I['corr'], writes=['corr'])
        p.dma('sp', rk[:], I['rk'], writes=['rk'])
        CONST = ['ident_f', 'ident_b', 'ones_b', 'ones_f', 'zcol', 'onecol', 'maskT', 'maskb', 'corr']

        psA = Rot([(psum(f"psA{i}", [128, 512]), f"psA{i}") for i in range(4)])
        psY = psum("psY", [128, 1024])
        psTb = psum("psTb", [128, 1024], BF16)
        psO = [(psum(f"psO{i}", [128, 512]), f"psO{i}") for i in range(1)]
        psX = Rot(psA.items + [psO[0], (psY[:, 0:512], 'psY'), (psY[:, 512:1024], 'psY2')])
        PSW = [psA]

        xT = sb("xT", [128, 8, T])
        uT = sb("uT", [128, 8, T], BF16)
        ypT = sb("ypT", [128, 4, T], BF16)
        atT = sb("atT", [128, 4, T], BF16)
        ysT = sb("ysT", [128, 8, T], BF16)
        NWS = 4
        wsl = [sb(f"wsl{i}", [128, 4096], BF16) for i in range(NWS)]
        Dsb = sb("Dsb", [128, 2048])
        tAB = sb("tAB", [128, 2048])
        P1 = [sb(f"P1_{l}", [128, 88]) for l in range(DEPTH)]
        P2 = [sb(f"P2_{l}", [128, 88]) for l in range(DEPTH)]
        P3 = [sb(f"P3_{l}", [128, 88]) for l in range(DEPTH)]
        PF = tAB[:, 1024:2048]
        negfb = sb("negfb", [8, DEPTH])
        dtb = sb("dtb", [16, DEPTH])
        negA = sb("negA", [16, DEPTH])
        Dvec = [sb(f"Dvec{l}", [128, 16]) for l in range(DEPTH)]
        pstg = sb("pstg", [128, 128])
        pcar = [sb(f"pcar{l}", [128, 4, 16]) for l in range(DEPTH)]
        scar = [sb(f"scar{l}", [128, 12, 3]) for l in range(DEPTH)]
        scarB = [sb(f"scarB{l}", [128, 12, 3]) for l in range(DEPTH)]
        fcar = [sb(f"fcar{l}", [128, 44, 2]) for l in range(DEPTH)]
        offtab = [sb(f"offtab{l}", [8, NT]) for l in range(DEPTH)]
        _sst = sb("Sst", [128, 1024])
        _sbf = sb("Sbf", [128, 1024], BF16)
        Sst = [_sst for l in range(DEPTH)]
        Sbf = [_sbf for l in range(DEPTH)]
        for l in range(DEPTH):
            MS(pcar[l][:], 0.0, [f'pcar{l}'])
            MS(scar[l][:], 0.0, [f'scar{l}'])
            MS(fcar[l][:], 0.0, [f'fcar{l}'])
            MS(offtab[l][:], 0.0, [f'offtab{l}'])
            MS(Sst[l][:], 0.0, ['Sst'])
            MS(Sbf[l][:], 0.0, ['Sbf'])

        def load_rows(rows, dst, dname):
            off = 0
            for ap, n in rows:
                p.dma('sp', pstg[off:off + n, :], ap, writes=['pstg'])
                off += n
            ps, pn = psA.next()
            TR(ps[:, 0:off], pstg[0:off, :], ident_f[0:off, 0:off], ['pstg', 'ident_f'], [pn])
            CP(dst[:, 0:off], ps[:, 0:off], [pn], [dname])

        def r128(ap1d):
            return ap1d.rearrange("(c p) -> c p", p=128)

        for l in range(DEPTH):
            load_rows([(r128(I['norm_mix'][l]), 8), (r128(I['norm_ffn'][l]), 8), (r128(I['ssd_norm'][l]), 8),
                       (r128(I['pool_scale'][l]), 4)] + [(r128(I['ssd_conv_w'][l][k]), 12) for k in range(4)]
                      + [(r128(I['ssd_conv_b'][l]), 12)], P1[l], f'P1_{l}')
            load_rows([(r128(I['ffn_conv_w'][l][0]), 44), (r128(I['ffn_conv_w'][l][1]), 44)], P2[l], f'P2_{l}')
            load_rows([(r128(I['ffn_conv_w'][l][2]), 44), (r128(I['ffn_conv_b'][l]), 44)], P3[l], f'P3_{l}')
            p.dma('sp', negfb[:, l:l + 1], I['f_bias'][l].rearrange("(h o) -> h o", o=1), writes=['negfb'])
            p.dma('sp', dtb[:, l:l + 1], I['ssd_dt_bias'][l].rearrange("(h o) -> h o", o=1), writes=['dtb'])
            p.dma('sp', negA[:, l:l + 1], I['ssd_a_log'][l].rearrange("(h o) -> h o", o=1), writes=['negA'])
            p.dma('sp', Dvec[l][:], I['ssd_d'][l].partition_broadcast(128), writes=[f'Dvec{l}'])
        TS(negfb[:], negfb[:], -1.0, None, ALU.mult, None, ['negfb'], ['negfb'])
        ACT(negA[:], negA[:], AF.Exp, ['negA'], ['negA'])
        TS(negA[:], negA[:], -1.0, None, ALU.mult, None, ['negA'], ['negA'])
        PAR = ['negfb', 'dtb', 'negA'] + [f'{a}_{l}' for a in ('P1', 'P2', 'P3') for l in range(DEPTH)] + [f'Dvec{l}' for l in range(DEPTH)]

        TAG[0] = 'prepass'
        WSRES = {}
        lp = layer_plan(NSL)
        with ExitStack() as pst:
            NST = 8
            pstA = [(pst.enter_context(nc.sbuf_tensor(f"pstA{i}", [128, 2048], F32)), f"pstA{i}") for i in range(NST)]
            pstB = [(pst.enter_context(nc.sbuf_tensor(f"pstB{i}", [128, 2048], BF16)), f"pstB{i}") for i in range(NST)]
            cast_eng = Rot(['dve', 'act', 'pool', 'dve', 'act'])
            units = []
            for l in range(DEPTH):
                seen = set()
                for nm in lp:
                    if nm in seen:
                        continue
                    seen.add(nm)
                    src, kc, segs = bdefs[l][nm]
                    ncols = sum(n_ for _, n_ in segs)
                    cstep = max(1, 2048 // ncols)
                    WSRES[(l, nm)] = []
                    for c0 in range(0, kc, cstep):
                        units.append((l, nm, c0, min(kc, c0 + cstep)))

            def pc_in(ui):
                l, nm, c0, c1 = units[ui]
                src, kc, segs = bdefs[l][nm]
                ncols = sum(n_ for _, n_ in segs)
                sg, sgn = pstA[ui % NST]
                nel = (c1 - c0) * ncols
                sv = sg[:, 0:nel].rearrange("p (c n) -> p c n", c=c1 - c0)
                off = 0
                for col, n_ in segs:
                    p.dma('sp', sv[:, :, off:off + n_], src[c0 * 128:c1 * 128, col:col + n_].rearrange("(c p) n -> p c n", p=128), writes=[sgn])
                    off += n_

            LAP = 7
            for ui in range(min(LAP, len(units))):
                pc_in(ui)
            for ui in range(len(units)):
                if ui + LAP < len(units):
                    pc_in(ui + LAP)
                l, nm, c0, c1 = units[ui]
                src, kc, segs = bdefs[l][nm]
                ncols = sum(n_ for _, n_ in segs)
                dstv = WS[(l, nm)].rearrange("p (c n) -> p c n", c=kc)
                sg, sgn = pstA[ui % NST]
                wv, wvn = pstB[ui % NST]
                nel = (c1 - c0) * ncols
                ce = cast_eng.next()
                if ce == 'act':
                    p.op('act', lambda e, o=wv[:, 0:nel], i=sg[:, 0:nel]: e.copy(out=o, in_=i), [sgn], [wvn])
                else:
                    CP(wv[:, 0:nel], sg[:, 0:nel], [sgn], [wvn], eng=ce)
                rn = ('ws', l, nm, c0)
                p.dma('sp', dstv[:, c0:c1, :], wv[:, 0:nel].rearrange("p (c n) -> p c n", c=c1 - c0), reads=[wvn], writes=[rn])
                WSRES[(l, nm)].append(rn)
            p.barrier()

        plan = [(l, nm) for l in range(DEPTH) for nm in lp]
        wstate = {'i': 0, 'issued': 0}

        def w_issue():
            k = wstate['issued']
            if k < len(plan):
                l, nm = plan[k]
                src, kc, segs = bdefs[l][nm]
                nel = kc * sum(n for _, n in segs)
                p.dma('sp', wsl[k % NWS][:, 0:nel], WS[(l, nm)], reads=WSRES[(l, nm)], writes=[f'w{k % NWS}'])
                wstate['issued'] += 1

        def w_get(l, nm):
            k = wstate['i']
            assert plan[k] == (l, nm), (plan[k], l, nm)
            src, kc, segs = bdefs[l][nm]
            ncols = sum(n for _, n in segs)
            return wsl[k % NWS][:, 0:kc * ncols].rearrange("p (c n) -> p c n", c=kc), f'w{k % NWS}'

        def w_rel():
            wstate['i'] += 1
            w_issue()

        for _ in range(NWS):
            w_issue()

        sq = Rot([(sb(f"sq{i}", [128, T], BF16), f"sq{i}") for i in range(2)])
        rstd = sb("rstd", [128, T])
        hb3 = Rot([(sb(f"hb{i}", [128, T + 16]), f"hb{i}") for i in range(3)])
        acc2 = Rot([(sb(f"acc{i}", [128, T]), f"acc{i}") for i in range(3)])
        sgt = [(sb(f"sg{i}", [128, T]), f"sg{i}") for i in range(4)]
        gt = Rot([(sb(f"gt{i}", [128, T]), f"gt{i}") for i in range(2)])
        xin = Rot([(Dsb[:, 0:1024], 'Dsb')])
        dT = Rot([(sb(f"dT{i}", [128, T], BF16), f"dT{i}") for i in range(2)])
        ssc = sb("ssc", [128, 4])
        qa = sb("qa", [128, 8, T], BF16)
        ka = sb("ka", [128, 8, T], BF16)
        vcur = sb("vcur", [128, NB, 8, 128], BF16)
        NKV = 3
        ksl = [sb(f"ksl{i}", [128, T], BF16) for i in range(NKV)]
        vsl = [sb(f"vsl{i}", [128, NB, 128], BF16) for i in range(NKV)]
        pTt = Rot([(sb(f"pT{i}", [128, T], BF16), f"pT{i}") for i in range(4)])
        rec = tAB[:, 1536:2048]
        fe = Dsb[0:8, 0:512]
        Cl = Dsb[0:8, 512:1024]
        r1 = Dsb[0:8, 1024:1536]
        hml = None
        dif8 = sb("dif8", [8, NT])
        BD = sb("BD", [8, 8, NT])
        biasb = sb("biasb", [128, 8, NT])
        MS(qa[64:70, :, :], 1.0, [('qaug', h_, r_) for h_ in range(8) for r_ in range(3)])
        MS(ka[64:70, :, :], 1.0, [('kaug', h_, r_) for h_ in range(8) for r_ in range(3)])
        MS(vcur[:], 1.0, [('v', h_) for h_ in range(8)])
        xc2 = Rot([(sb(f"xc{i}", [128, T], BF16), f"xc{i}") for i in range(2)])
        BCT = sb("BCT", [128, 4, T], BF16)
        xtok_t = sb("xtok", [128, NB * 1024], BF16)
        xtok = xtok_t[:].rearrange("p (b n) -> p b n", b=NB)
        mgT = xtok_t[:].rearrange("p (c n) -> p c n", c=8)
        Btok = sb("Btok", [128, NB, 256], BF16)
        big_t = sb("big", [128, 6144], BF16)
        zs = big_t[:, 0:4096].rearrange("p (b n) -> p b n", b=NB)
        EM = big_t[:, 4096:6144]
        actT = big_t[:].rearrange("p (c n) -> p c n", c=12)
        hml = big_t[0:8, 0:3072].rearrange("p (a r n) -> p a r n", a=2, r=3)
        xd = sb("xd", [128, 1024], BF16)
        xdd = sb("xdd", [128, 1024], BF16)
        ygn = sb("ygn", [128, 1024], BF16)
        CBs = sb("CBs", [128, 256])
        dtT = gt.items[0][0][0:16, :]
        aT = gt.items[1][0][0:16, :]
        acsT = sgt[0][0][0:16, :]
        Zbd = tAB[0:16, :].rearrange("p (h l) -> p h l", h=16)
        Zl = sb("Zl", [16, 16])
        tk = sb("tk", [128, 6, 16])

        def rmsnorm_to_uT(wcol_tile, wcol_name, wcol_off):
            ps, pn = psA.next()
            for c in range(8):
                s_, sn = sq.next()
                ACT(s_[:], xT[:, c, :], AF.Square, ['xT'], [sn])
                MM(ps[:], ones_b[:], s_[:], c == 0, c == 7, [sn, 'ones_b'], [pn])
            ACT(rstd[:], ps[:], AF.Sqrt, [pn, 'epscol'], ['rstd'], bias=epscol[:], scale=1.0 / D)
            p.op('dve', lambda e: e.reciprocal(out=rstd[:], in_=rstd[:]), ['rstd'], ['rstd'])
            for c in range(8):
                STT(uT[:, c, :], xT[:, c, :], wcol_tile[:, wcol_off + c:wcol_off + c + 1], rstd[:], ALU.mult, ALU.mult,
                    ['xT', 'rstd', wcol_name], ['uT'])

        def proj(w, wn, col0, ncols, src, srcn, kc):
            ps, pn = PSW[0].next()
            for c in range(kc):
                MM(ps[0:ncols, :], w[:, c, col0:col0 + ncols], src[:, c, :], c == 0, c == kc - 1, [wn, srcn], [pn])
            return ps, pn

        def conv_chunk(ps, pn, ntap, car, carn, ci, wt, wtn, wcols, bcol):
            H = ntap - 1
            hb, hn = hb3.next()
            a, an = acc2.next()
            carn = (carn, ci)
            hc, hm = (hn, 'c'), (hn, 'm')
            CP(hb[:, 0:H], car[:, ci, :], [carn], [hc])
            ACT(hb[:, H:H + T], ps[:], AF.Copy, [pn], [hm])
            ACT(a[:], ps[:], AF.Identity, [pn, wtn[-1]], [an], bias=bcol, scale=wcols[ntap - 1])
            CP(car[:, ci, :], hb[:, T:T + H], [hm], [carn])
            for k in range(ntap - 1):
                STT(a[:], hb[:, k:k + T], wcols[k], a[:], ALU.mult, ALU.add, [hc, hm, an] + wtn, [an])
            return a, an


        G = 3 * NSL

        def load_xT(s_):
            p.dma('sp', xT[:], XRES[s_].rearrange("p (c n) -> p c n", c=8), reads=[('xres', s_)], writes=['xT'])

        def store_xT(s_):
            p.dma('sp', XRES[s_].rearrange("p (c n) -> p c n", c=8), xT[:], reads=['xT'], writes=[('xres', s_)])

        def tok_major_block(b, dst, dstn):
            for cq in range(2):
                ps, pn = psA.next()
                for k in range(4):
                    c = cq * 4 + k
                    TR(ps[:, k * 128:(k + 1) * 128], xT[:, c, b * 128:(b + 1) * 128], ident_f[:], ['xT', 'ident_f'], [pn])
                ACT(dst[:, cq * 512:(cq + 1) * 512], ps[:], AF.Copy, [pn], [dstn])

        def halo_from(l, src_dram_or_none, gathered):
            ht = tAB[0:16, 1024:2048]
            if not gathered:
                p.dma('sp', ht, src_dram_or_none, writes=['tAB'])
            else:
                MS(ht, 0.0, ['tAB'], eng='dve')
                for q in range(4):
                    p.dma('sp', tAB[0:16, 0:1024], src_dram_or_none[q * 16:(q + 1) * 16, :], reads=['hg'], writes=['tAB'])
                    STT(ht, tAB[0:16, 0:1024], rk[0:16, 8 + q:9 + q], ht, ALU.mult, ALU.add, ['tAB', 'rk'], ['tAB'])
            ps, pn = psA.next()
            for c in range(8):
                TR(ps[:, c * 16:(c + 1) * 16], tAB[0:16, 1024 + c * 128:1024 + (c + 1) * 128], ident_f[0:16, 0:16], ['tAB', 'ident_f'], [pn])
            CP(xhT[:], ps[:, 0:128].rearrange("p (c n) -> p c n", c=8), [pn], ['xhT'])

        def norm_small(wcol_tile, wcol_name, wcol_off):
            ps, pn = psA.next()
            for c in range(8):
                s_, sn = sq.next()
                ACT(s_[:, 0:16], xhT[:, c, :], AF.Square, ['xhT'], [sn])
                MM(ps[:, 0:16], ones_b[:], s_[:, 0:16], c == 0, c == 7, [sn, 'ones_b'], [pn])
            ACT(rstd[:, 0:16], ps[:, 0:16], AF.Sqrt, [pn, 'epscol'], ['rstd'], bias=epscol[:], scale=1.0 / D)
            p.op('dve', lambda e: e.reciprocal(out=rstd[:, 0:16], in_=rstd[:, 0:16]), ['rstd'], ['rstd'])
            for c in range(8):
                STT(uhT[:, c, :], xhT[:, c, :], wcol_tile[:, wcol_off + c:wcol_off + c + 1], rstd[:, 0:16], ALU.mult, ALU.mult,
                    ['xhT', 'rstd', wcol_name], ['uhT'])

        def proj16(w, wn, col0, ncols):
            ps, pn = psA.next()
            for c in range(8):
                MM(ps[0:ncols, 0:16], w[:, c, col0:col0 + ncols], uhT[:, c, :], c == 0, c == 7, [wn, 'uhT'], [pn])
            return ps, pn

        def attn_f(l, s, phase):
            w, wn = w_get(l, 'in_f')
            ps, pn = psA.next()
            for c in range(8):
                MM(ps[0:8, :], w[:, c, 0:8], uT[:, c, :], c == 0, c == 7, [wn, 'uT'], [pn])
            w_rel()
            ACT(fe, ps[0:8, :], AF.Exp, [pn, 'negfb'], ['Dsb'], bias=negfb[:, l:l + 1], scale=-1.0)
            ACT(fe, fe, AF.Ln, ['Dsb', 'onecol'], ['Dsb'], bias=onecol[0:8, :], scale=1.0)
            SCAN(Cl, ones_f[0:8, 0:T], fe, 0.0, ['Dsb', 'ones_f'], ['Dsb'])
            if phase == 'A':
                TT(offtab[l][:, G + s + 1:G + s + 2], offtab[l][:, G + s:G + s + 1], Cl[:, T - 1:T], ALU.add, ['Dsb', f'offtab{l}'], [f'offtab{l}'])
            CP(hml[:, 1, 0, :], Cl, ['Dsb'], ['zs'])
            TT(r1, Cl, hml[:, 1, 0, :], ALU.subtract, ['Dsb', 'zs'], ['Dsb'])
            CP(hml[:, 1, 1, :], r1, ['Dsb'], ['zs'])
            TT(r1, r1, hml[:, 1, 1, :], ALU.subtract, ['Dsb', 'zs'], ['Dsb'])
            CP(hml[:, 1, 2, :], r1, ['Dsb'], ['zs'])
            if phase == 'B':
                TS(hml[:, 0, :, :], hml[:, 1, :, :], -1.0, None, ALU.mult, None, ['zs'], ['zs'])
            for h in range(8):
                for r in range(3):
                    if phase == 'B':
                        p.dma('sp', qa[64 + r:65 + r, h, :], hml[h:h + 1, 0, r, :], reads=['zs'], writes=[('qaug', h, r)])
                    else:
                        p.dma('sp', ka[67 + r:68 + r, h, :], hml[h:h + 1, 1, r, :], reads=['zs'], writes=[('kaug', h, r)])

        def ssd_prep(l, s, phase, car, carn):
            if phase == 'B':
                for zb in range(2):
                    w, wn = w_get(l, f'in_z{zb}')
                    for b in range(NB):
                        ps, pn = psA.next()
                        for c in range(8):
                            MM(ps[:], uT[:, c, b * 128:(b + 1) * 128], w[:, c, :], c == 0, c == 7, [wn, 'uT'], [pn])
                        ACT(zs[:, b, zb * 512:(zb + 1) * 512], ps[:], AF.Silu, [pn], ['zs'])
                    w_rel()
            if phase == 'B':
                p.dma('sp', xtok, XSP[s].rearrange("p (b n) -> p b n", b=NB), reads=[('xsp', s)], writes=['xtok'])
                p.dma('sp', Btok[:], BSP[s].rearrange("p (b n) -> p b n", b=NB), reads=[('bsp', s)], writes=['Btok'])
                p.dma('sp', BCT[:], CSP[s].rearrange("p (c n) -> p c n", c=4), reads=[('csp', s)], writes=['BCT'])
                p.dma('sp', dtT, DSP[s][0:16, :], reads=[('dsp0', s)], writes=['gt0'])
                p.dma('sp', acsT, DSP[s][16:32, :], reads=[('dsp1', s)], writes=['sg0'])
                return
            pend_tr = []
            pend_silu = []
            for xb in range(3):
                w, wn = w_get(l, f'in_x{xb}')
                for k4 in range(4):
                    ci = xb * 4 + k4
                    ps, pn = proj(w, wn, k4 * 128, 128, uT, 'uT', 8)
                    wc = [P1[l][:, 28 + k * 12 + ci:29 + k * 12 + ci] for k in range(4)]
                    a, an = conv_chunk(ps, pn, 4, car, carn, ci, P1[l], [f'P1_{l}'], wc, P1[l][:, 76 + ci:77 + ci])
                    while pend_silu:
                        pend_silu.pop(0)()
                    if pend_tr:
                        pend_tr.pop(0)()
                    if ci < 10:
                        if ci < 8:
                            xc, xcn = xc2.next()
                            pend_silu.append(lambda xc=xc, a=a, an=an, xcn=xcn: ACT(xc[:], a[:], AF.Silu, [an], [xcn]))
                            src_ap = xc
                        else:
                            xcn = 'BCT'
                            pend_silu.append(lambda ci=ci, a=a, an=an, xcn=xcn: ACT(BCT[:, ci - 8, :], a[:], AF.Silu, [an], [xcn]))
                            src_ap = BCT[:, ci - 8, :]

                        def do_tr(ci=ci, src_ap=src_ap, xcn=xcn):
                            po = (ci % 2) * 512
                            pbn = f'psTb{ci % 2}'
                            for b in range(NB):
                                TR(psTb[:, po + b * 128:po + (b + 1) * 128], src_ap[:, b * 128:(b + 1) * 128], ident_b[:], [xcn, 'ident_b'], [pbn])
                            if ci < 8:
                                p.op('act', lambda e, o_=xtok[:, :, ci * 128:(ci + 1) * 128], i_=psTb[:, po:po + 512].rearrange("p (b n) -> p b n", b=NB): e.copy(out=o_, in_=i_), [pbn], ['xtok'])
                            else:
                                p.op('act', lambda e, o_=Btok[:, :, (ci - 8) * 128:(ci - 7) * 128], i_=psTb[:, po:po + 512].rearrange("p (b n) -> p b n", b=NB): e.copy(out=o_, in_=i_), [pbn], ['Btok'])
                        pend_tr.append(do_tr)
                    else:
                        pend_silu.append(lambda ci=ci, a=a, an=an: ACT(BCT[:, ci - 8, :], a[:], AF.Silu, [an], ['BCT']))
                w_rel()
            while pend_silu:
                pend_silu.pop(0)()
            while pend_tr:
                pend_tr.pop(0)()
            w, wn = w_get(l, 'in_dt')
            ps, pn = psA.next()
            for c in range(8):
                MM(ps[0:16, :], w[:, c, 0:16], uT[:, c, :], c == 0, c == 7, [wn, 'uT'], [pn])
            w_rel()
            ACT(dtT, ps[0:16, :], AF.Exp, [pn, 'dtb'], ['gt0'], bias=dtb[:, l:l + 1], scale=1.0)
            ACT(dtT, dtT, AF.Ln, ['gt0', 'onecol'], ['gt0'], bias=onecol[0:16, :], scale=1.0)
            TS(aT, dtT, negA[:, l:l + 1], None, ALU.mult, None, ['gt0', 'negA'], ['gt1'])
            for b in range(NB):
                SCAN(acsT[:, b * 128:(b + 1) * 128], ones_f[0:16, 0:128], aT[:, b * 128:(b + 1) * 128], 0.0, ['gt1', 'ones_f'], ['sg0'])
            p.dma('sp', XSP[s].rearrange("p (b n) -> p b n", b=NB), xtok, reads=['xtok'], writes=[('xsp', s)])
            p.dma('sp', BSP[s].rearrange("p (b n) -> p b n", b=NB), Btok[:], reads=['Btok'], writes=[('bsp', s)])
            p.dma('sp', CSP[s].rearrange("p (c n) -> p c n", c=4), BCT[:], reads=['BCT'], writes=[('csp', s)])
            p.dma('sp', DSP[s][0:16, :], dtT, reads=['gt0'], writes=[('dsp0', s)])
            p.dma('sp', DSP[s][16:32, :], acsT, reads=['sg0'], writes=[('dsp1', s)])

        def ssd_tables(b):
            bs = slice(b * 128, (b + 1) * 128)
            ps, pn = psA.next()
            TR(ps[:, 0:16], acsT[:, bs], ident_f[0:16, 0:16], ['sg0', 'ident_f'], [pn])
            TR(ps[:, 16:32], dtT[:, bs], ident_f[0:16, 0:16], ['gt0', 'ident_f'], [pn])
            TS(Zl[:], ident_f[0:16, 0:16], acsT[:, b * 128 + 127:b * 128 + 128], None, ALU.mult, None, ['sg0', 'ident_f'], ['Zl'])
            MM(ps[:, 32:48], ones_f[0:16, 0:128], Zl[:], True, True, ['Zl', 'ones_f'], [pn])
            CP(tk[:, 0:3, :], ps[:, 0:48].rearrange("p (a h) -> p a h", a=3), [pn], ['tk'])
            ACT(tk[:, 3, :], tk[:, 2, :], AF.Exp, ['tk'], ['tk'])
            TT(tk[:, 4, :], tk[:, 2, :], tk[:, 0, :], ALU.subtract, ['tk'], ['tk'])
            ACT(tk[:, 4, :], tk[:, 4, :], AF.Exp, ['tk'], ['tk'])
            ACT(tk[:, 5, :], tk[:, 0, :], AF.Exp, ['tk'], ['tk'])

        def ssd_xd(b, eng2='dve'):
            TT(xd[:].rearrange("p (h d) -> p h d", h=16), xtok[:, b, :].rearrange("p (h d) -> p h d", h=16),
               tk[:, 1, :].unsqueeze(2).to_broadcast([128, 16, 64]), ALU.mult, ['xtok', 'tk'], ['xd'])
            TT(xdd[:].rearrange("p (h d) -> p h d", h=16), xd[:].rearrange("p (h d) -> p h d", h=16),
               tk[:, 4, :].unsqueeze(2).to_broadcast([128, 16, 64]), ALU.mult, ['xd', 'tk'], ['xdd'], eng=eng2)

        def ssd_state(l, b, with_bf):
            for g in range(2):
                ps, pn = psA.next()
                MM(ps[:], Btok[:, b, g * 128:(g + 1) * 128], xdd[:, g * 512:(g + 1) * 512], True, True, ['Btok', 'xdd'], [pn])
                TT(Sst[l][:, g * 512:(g + 1) * 512].rearrange("p (h d) -> p h d", h=8), Sst[l][:, g * 512:(g + 1) * 512].rearrange("p (h d) -> p h d", h=8),
                   tk[:, 3, g * 8:(g + 1) * 8].unsqueeze(2).to_broadcast([128, 8, 64]), ALU.mult, ['Sst', 'tk'], ['Sst'])
                TT(Sst[l][:, g * 512:(g + 1) * 512], Sst[l][:, g * 512:(g + 1) * 512], ps[:], ALU.add, ['Sst', pn], ['Sst'])
            if with_bf:
                p.op('act', lambda e, o_=Sbf[l][:], i_=Sst[l][:]: e.copy(out=o_, in_=i_), ['Sst'], ['Sbf'])

        TAG[0] = 'prepass'
        for s in range(NSL):
            for b in range(NB):
                xi, xn = xin.next()
                p.dma('sp', xi, I['x'][s * T + b * 128:s * T + (b + 1) * 128, :], writes=[xn])
                for cq in range(2):
                    ps, pn = psA.next()
                    for k in range(4):
                        c = cq * 4 + k
                        TR(ps[:, k * 128:(k + 1) * 128], xi[:, c * 128:(c + 1) * 128], ident_f[:], [xn, 'ident_f'], [pn])
                    p.op('act', lambda e, o=xT[:, cq * 4:cq * 4 + 4, b * 128:(b + 1) * 128], i=ps[:].rearrange("p (k n) -> p k n", k=4): e.copy(out=o, in_=i), [pn], ['xT'])
            store_xT(s)

        for l in range(DEPTH):
            TAG[0] = 'haloA'
            if l == 0:
                halo_from(l, I['xh0'], False)
            else:
                halo_from(l, HG, True)
            norm_small(P1[l], f'P1_{l}', 0)
            w, wn = w_get(l, 'in_pool')
            for g in range(4):
                ps, pn = proj16(w, wn, g * 128, 128)
                CP(pcar[l][:, g, :], ps[:, 0:16], [pn], [(f'pcar{l}', g)])
            w_rel()
            for xb in range(3):
                w, wn = w_get(l, f'in_x{xb}')
                for k4 in range(4):
                    ci = xb * 4 + k4
                    ps, pn = proj16(w, wn, k4 * 128, 128)
                    CP(scar[l][:, ci, :], ps[:, 13:16], [pn], [(f'scar{l}', ci)])
                    CP(scarB[l][:, ci, :], ps[:, 13:16], [pn], [f'scarB{l}'])
                w_rel()
            MS(Sst[l][:], 0.0, ['Sst'])
            MS(runacc[:], 0.0, ['runacc'], eng='dve')

            for s in range(NSL):
                TAG[0] = 'A_kv'
                PSW[0] = psX
                if s == 0:
                    load_xT(s)
                rmsnorm_to_uT(P1[l], f'P1_{l}', 0)
                load_xT(s + 1 if s + 1 < NSL else 0)
                w, wn = w_get(l, 'in_k')
                for j in range(4):
                    ps, pn = proj(w, wn, j * 128, 128, uT, 'uT', 8)
                    CP(ka[0:64, 2 * j, :], ps[0:64, :], [pn], [('ka', 2 * j)])
                    CP(ka[0:64, 2 * j + 1, :], ps[64:128, :], [pn], [('ka', 2 * j + 1)])
                w_rel()
                w, wn = w_get(l, 'in_v')
                for b in range(NB):
                    ps, pn = psA.next()
                    for c in range(8):
                        MM(ps[:], uT[:, c, b * 128:(b + 1) * 128], w[:, c, :], c == 0, c == 7, [wn, 'uT'], [pn])
                    p.op('act', lambda e, o=vcur[:, b, :, 0:64], i=ps[:].rearrange("p (h d) -> p h d", h=8): e.copy(out=o, in_=i), [pn], [('v', h_) for h_ in range(8)])
                w_rel()
                attn_f(l, s, 'A')
                for h in range(8):
                    p.dma('sp', KSr(h, s), ka[0:70, h, :], reads=[('ka', h)] + [('kaug', h, r_) for r_ in range(3)], writes=[('ks', h, s)])
                    p.dma('sp', VSr(h, s).rearrange("p (b n) -> p b n", b=NB), vcur[:, :, h, :], reads=[('v', h)], writes=[('vs', h, s)])
                TAG[0] = 'A_ssd'
                ssd_prep(l, s, 'A', scar[l], f'scar{l}')
                TAG[0] = 'A_chunk'
                for b in range(NB):
                    ssd_tables(b)
                    ssd_xd(b)
                    ssd_state(l, b, False)
                    TT(runacc[:], runacc[:], tk[:, 2, :], ALU.add, ['runacc', 'tk'], ['runacc'])
            PSW[0] = psA
            TAG[0] = 'exch'
            p.dma('sp', FS[0:128, :], Sst[l][:], reads=['Sst'], writes=['fs'])
            p.dma('sp', FS[128:129, 0:16], runacc[0:1, :], reads=['runacc'], writes=['fs'])
            p.dma('sp', FS[129:130, 0:8 * (NSL + 1)].rearrange("o (h j) -> (o h) j", h=8), offtab[l][:, G:G + NSL + 1], reads=[f'offtab{l}'], writes=['fs'])
            for s_ in range(NSL):
                ksr = [('ks', h, s_) for h in range(8)]
                vsr = [('vs', h, s_) for h in range(8)]
                p.coll(dict(kind="AllGather", op=ALU.bypass, replica_groups=RG, ins=[KSs[s_].opt()], outs=[KGs[s_].opt()]), reads=ksr, writes=[('kg', s_)])
                p.coll(dict(kind="AllGather", op=ALU.bypass, replica_groups=RG, ins=[VSs[s_].opt()], outs=[VGs[s_].opt()]), reads=vsr, writes=[('vg', s_)])
            p.coll(dict(kind="AllGather", op=ALU.bypass, replica_groups=RG, ins=[FS.opt()], outs=[FG.opt()]), reads=['fs'], writes=['fg'])

            MS(Sst[l][:], 0.0, ['Sst'])
            for q in range(3):
                p.dma('sp', tk[:, 0, :], FG[q * RF + 128, 0:16].partition_broadcast(128), reads=['fg'], writes=['tk'])
                ACT(tk[:, 1, :], tk[:, 0, :], AF.Exp, ['tk'], ['tk'])
                TS(tk[:, 2, :], tk[:, 1, :], rk[:, q:q + 1], rk[:, 12 + q:13 + q], ALU.mult, ALU.add, ['tk', 'rk'], ['tk'])
                p.dma('sp', Dsb[:, 0:1024], FG[q * RF:q * RF + 128, :], reads=['fg'], writes=['Dsb'])
                TT(Sst[l][:].rearrange("p (h d) -> p h d", h=16), Sst[l][:].rearrange("p (h d) -> p h d", h=16),
                   tk[:, 2, :].unsqueeze(2).to_broadcast([128, 16, 64]), ALU.mult, ['Sst', 'tk'], ['Sst'])
                STT(Sst[l][:], Dsb[:, 0:1024], rk[:, q:q + 1], Sst[l][:], ALU.mult, ALU.add, ['Dsb', 'rk', 'Sst'], ['Sst'])
            p.op('act', lambda e, o_=Sbf[l][:], i_=Sst[l][:]: e.copy(out=o_, in_=i_), ['Sst'], ['Sbf'])
            for q in range(4):
                p.dma('sp', goff[:, q, :], FG[q * RF + 129:q * RF + 130, 0:8 * (NSL + 1)].rearrange("o (h j) -> (o h) j", h=8), reads=['fg'], writes=['goff'])
            TT(gtm[:], goff[:, :, NSL], rk[0:8, 0:4], ALU.mult, ['goff', 'rk'], ['gtm'])
            for q in (2, 1, 0):
                TT(gtm[:, q:q + 1], gtm[:, q:q + 1], gtm[:, q + 1:q + 2], ALU.add, ['gtm'], ['gtm'])
            TT(gtm[:], rk[0:8, 4:8], gtm[:], ALU.subtract, ['gtm', 'rk'], ['gtm'])
            TT(offtab[l][:, 0:G].rearrange("p (q j) -> p q j", q=3), goff[:, 0:3, 0:NSL],
               gtm[:, 0:3].unsqueeze(2).to_broadcast([8, 3, NSL]), ALU.add, ['goff', 'gtm'], [f'offtab{l}'])

            for s in range(NSL):
                TAG[0] = 'B_pool'
                if s > 0:
                    load_xT(s)
                rmsnorm_to_uT(P1[l], f'P1_{l}', 0)
                w, wn = w_get(l, 'in_pool')
                wstate['i'] += 1
                mw, mwn = w_get(l, 'mixw')
                wstate['i'] -= 1
                pend_mix = []
                for g in range(4):
                    ps, pn = proj(w, wn, g * 128, 128, uT, 'uT', 8)
                    hb, hn = hb3.next()
                    hc, hm = (hn, 'c'), (hn, 'm')
                    CP(hb[:, 0:16], pcar[l][:, g, :], [(f'pcar{l}', g)], [hc])
                    ACT(hb[:, 16:16 + T], ps[:], AF.Copy, [pn], [hm])
                    CP(pcar[l][:, g, :], hb[:, T:T + 16], [hm], [(f'pcar{l}', g)])
                    cur, curn = hb, hn
                    pp = [(Dsb, 'Dsb'), (tAB, 'tAB')]
                    for j in range(g + 1):
                        sh = 1 << j
                        nx, nxn = pp[j % 2]
                        TT(nx[:, sh:T + 16], cur[:, sh:T + 16], cur[:, 0:T + 16 - sh], ALU.add, ([hc, hm] if j == 0 else [curn]), [nxn])
                        cur, curn = nx, nxn
                    wlen = 2 << g
                    if s == 0:
                        TT(cur[:, 16:32], cur[:, 16:32], corr[:, g * 16:(g + 1) * 16], ALU.mult, [curn, 'corr'], [curn])
                    d_, dn = dT.next()
                    STT(d_[:], cur[:, 16:16 + T], 1.0 / wlen, hb[:, 16:16 + T], ALU.mult, ALU.subtract, [curn, hm], [dn])
                    def do_mix(g=g, d_=d_, dn=dn):
                        psm, pmn = psA.next()
                        MM(psm[:], mw[:, g, :], d_[:], True, True, [mwn, dn], [pmn])
                        TS(ypT[:, g, :], psm[:], P1[l][:, 24 + g:25 + g], None, ALU.mult, None, [pmn, f'P1_{l}'], ['ypT'])
                    pend_mix.append(do_mix)
                    if len(pend_mix) > 1:
                        pend_mix.pop(0)()
                while pend_mix:
                    pend_mix.pop(0)()
                w_rel()
                w_rel()

                TAG[0] = 'B_attnprep'
                w, wn = w_get(l, 'in_q')
                for j in range(4):
                    ps, pn = proj(w, wn, j * 128, 128, uT, 'uT', 8)
                    ACT(qa[0:64, 2 * j, :], ps[0:64, :], AF.Identity, [pn, 'zcol'], [('qa', 2 * j)], bias=zcol[0:64, :], scale=0.125)
                    ACT(qa[0:64, 2 * j + 1, :], ps[64:128, :], AF.Identity, [pn, 'zcol'], [('qa', 2 * j + 1)], bias=zcol[0:64, :], scale=0.125)
                w_rel()
                attn_f(l, s, 'B')
                for h in range(8):
                    p.dma('sp', ka[0:70, h, :], KSr(h, s), reads=[('ks', h, s)], writes=[('ka', h)] + [('kaug', h, r_) for r_ in range(3)])
                    p.dma('sp', vcur[:, :, h, :], VSr(h, s).rearrange("p (b n) -> p b n", b=NB), reads=[('vs', h, s)], writes=[('v', h)])
                nu = G + s
                TS(dif8[:, 0:nu], offtab[l][:, 0:nu], offtab[l][:, nu:nu + 1], None, ALU.subtract, None, [f'offtab{l}'], ['dif8'])
                TT(BD[:, :, 0:nu], dif8[:, 0:nu].unsqueeze(1).to_broadcast([8, 8, nu]), ident_f[0:8, 0:8].unsqueeze(2).to_broadcast([8, 8, nu]), ALU.mult, ['dif8', 'ident_f'], ['BD'])
                ps, pn = psA.next()
                for h in range(8):
                    MM(ps[:, h * nu:(h + 1) * nu], ones_f[0:8, 0:128], BD[:, h, 0:nu], True, True, ['BD', 'ones_f'], [pn])
                CP(biasb[:, :, 0:nu], ps[:, 0:8 * nu].rearrange("p (h j) -> p h j", h=8), [pn], ['biasb'])
                units = [(h, u) for h in range(8) for u in range(nu)]
                ust = {'issued': 0}

                def kv_issue():
                    k = ust['issued']
                    if k < len(units):
                        h_, u_ = units[k]
                        if u_ < G:
                            q_, j_ = u_ // NSL, u_ % NSL
                            r0 = q_ * 8 + h_
                            ksrc, rk_ = KGs[j_][r0 * 70:(r0 + 1) * 70, :], [('kg', j_)]
                            vsrc, rv_ = VGs[j_][r0 * 128:(r0 + 1) * 128, :], [('vg', j_)]
                        else:
                            j_ = u_ - G
                            ksrc, rk_ = KSr(h_, j_), [('ks', h_, j_)]
                            vsrc, rv_ = VSr(h_, j_), [('vs', h_, j_)]
                        p.dma('sp', ksl[k % NKV][0:70, :], ksrc, reads=rk_, writes=[f'ksl{k % NKV}'])
                        p.dma('sp', vsl[k % NKV][:], vsrc.rearrange("p (b n) -> p b n", b=NB), reads=rv_, writes=[f'vsl{k % NKV}'])
                        ust['issued'] += 1

                for _ in range(NKV - 1):
                    kv_issue()
                ucount = 0
                TAG[0] = 'B_attn'
                flat = []
                for h in range(8):
                    groups = []
                    for u in range(nu):
                        kslot = ucount % NKV
                        for b in range(NB):
                            groups.append(('prev', u, b, kslot))
                        ucount += 1
                    for b in range(NB):
                        groups.append(('diag', s, b, None))
                    ng = len(groups)
                    for gi, (kind, j, b, kslot) in enumerate(groups):
                        flat.append((h, gi, ng, kind, j, b, kslot))
                obanks = [(psO[0][0], psO[0][1]), (psY[:, 0:512], 'psY')]
                LA = 2
                pend = []

                def emit_pv(item):
                    h, gi, ng, lv, rdv, pt, ptn, col0 = item
                    o, on = obanks[h % 2]
                    MM(o[:, col0:T], lv, pt[:, col0:T], gi == 0, gi == ng - 1, rdv + [ptn], [on])
                    if gi == ng - 1:
                        p.op('dve', lambda e, o_=rec[64:128, :], i_=o[64:128, :]: e.reciprocal(out=o_, in_=i_), [on], ['tAB'])
                        hp = (h % 2) * 64
                        TT(atT[hp:hp + 64, h // 2, :], o[0:64, :], rec[64:128, :], ALU.mult, [on, 'tAB'], ['atT'])

                for (h, gi, ng, kind, j, b, kslot) in flat:
                    if kind == 'prev':
                        lk = ksl[kslot][0:70, b * 128:(b + 1) * 128]
                        lv = vsl[kslot][:, b, :]
                        rdk, rdv = [f'ksl{kslot}'], [f'vsl{kslot}']
                        col0 = 0
                        bias_ap = biasb[:, h, j:j + 1]
                        rdb = ['biasb']
                    else:
                        lk = ka[0:70, h, b * 128:(b + 1) * 128]
                        lv = vcur[:, b, h, :]
                        rdk, rdv = [('ka', h)] + [('kaug', h, r_) for r_ in range(3)], [('v', h)]
                        col0 = b * 128
                        bias_ap = zcol[:]
                        rdb = ['zcol']
                    ps, pn = psA.next()
                    MM(ps[:, col0:T], lk, qa[0:70, h, col0:T], True, True, rdk + [('qa', h)] + [('qaug', h, r_) for r_ in range(3)], [pn])
                    pt, ptn = pTt.next()
                    ACT(pt[:, col0:T], ps[:, col0:T], AF.Exp, [pn] + rdb, [ptn], bias=bias_ap, scale=1.0)
                    if kind == 'diag':
                        TT(pt[:, col0:col0 + 128], pt[:, col0:col0 + 128], maskT[:], ALU.mult, [ptn, 'maskT'], [ptn])
                    pend.append((h, gi, ng, lv, rdv, pt, ptn, col0))
                    if kind == 'prev' and b == NB - 1:
                        kv_issue()
                    if len(pend) > LA:
                        emit_pv(pend.pop(0))
                while pend:
                    emit_pv(pend.pop(0))

                TAG[0] = 'B_ssdprep'
                ssd_prep(l, s, 'B', scarB[l], f'scarB{l}')
                TAG[0] = 'B_ssd'
                for b in range(NB):
                    bs = slice(b * 128, (b + 1) * 128)
                    ssd_tables(b)
                    TT(Zbd[:], acsT[:, bs].unsqueeze(1).to_broadcast([16, 16, 128]), ident_f[0:16, 0:16].unsqueeze(2).to_broadcast([16, 16, 128]), ALU.mult, ['sg0', 'ident_f'], ['tAB'])
                    for pc in range(4):
                        ps, pn = psA.next()
                        MM(ps[:], ones_f[0:16, 0:128], Zbd[:, pc * 4:(pc + 1) * 4, :].rearrange("k h l -> k (h l)"), True, True, ['tAB', 'ones_f'], [pn])
                        TT(Dsb[:, pc * 512:(pc + 1) * 512].rearrange("p (h l) -> p h l", h=4), ps[:].rearrange("p (h l) -> p h l", h=4),
                           tk[:, 0, pc * 4:(pc + 1) * 4].unsqueeze(2).to_broadcast([128, 4, 128]), ALU.subtract, [pn, 'tk'], ['Dsb'])
                    TT(Dsb[:].rearrange("p (h l) -> p h l", h=16), Dsb[:].rearrange("p (h l) -> p h l", h=16),
                       maskb[:].unsqueeze(1).to_broadcast([128, 16, 128]), ALU.add, ['Dsb', 'maskb'], ['Dsb'])
                    ACT(EM[:], Dsb[:], AF.Exp, ['Dsb'], ['EM'])
                    ps, pn = psA.next()
                    for g in range(2):
                        MM(ps[:, g * 128:(g + 1) * 128], BCT[:, g, bs], BCT[:, 2 + g, bs], True, True, ['BCT'], [pn])
                    CP(CBs[:], ps[:, 0:256], [pn], ['CBs'])
                    TT(EM[:].rearrange("p (g e l) -> p g e l", g=2, e=8), EM[:].rearrange("p (g e l) -> p g e l", g=2, e=8),
                       CBs[:].rearrange("p (g l) -> p g l", g=2).unsqueeze(2).to_broadcast([128, 2, 8, 128]), ALU.mult, ['EM', 'CBs'], ['EM'])
                    ssd_xd(b, 'pool')
                    for h in range(16):
                        MM(psY[:, h * 64:(h + 1) * 64], EM[:, h * 128:(h + 1) * 128], xd[:, h * 64:(h + 1) * 64], True, True, ['EM', 'xd'], ['psY', 'psY2'])
                    for g in range(2):
                        ps, pn = psA.next()
                        MM(ps[:], BCT[:, 2 + g, bs], Sbf[l][:, g * 512:(g + 1) * 512], True, True, ['BCT', 'Sbf'], [pn])
                        TT(tAB[:, g * 512:(g + 1) * 512].rearrange("p (h d) -> p h d", h=8), ps[:].rearrange("p (h d) -> p h d", h=8),
                           tk[:, 5, g * 8:(g + 1) * 8].unsqueeze(2).to_broadcast([128, 8, 64]), ALU.mult, [pn, 'tk'], ['tAB'])
                    TT(tAB[:, 0:1024], tAB[:, 0:1024], psY[:], ALU.add, ['tAB', 'psY', 'psY2'], ['tAB'])
                    TT(Dsb[:, 0:1024].rearrange("p (h d) -> p h d", h=16), xtok[:, b, :].rearrange("p (h d) -> p h d", h=16),
                       Dvec[l][:].unsqueeze(2).to_broadcast([128, 16, 64]), ALU.mult, ['xtok', f'Dvec{l}'], ['Dsb'], eng='pool')
                    TT(tAB[:, 0:1024], tAB[:, 0:1024], Dsb[:, 0:1024], ALU.add, ['tAB', 'Dsb'], ['tAB'])
                    TT(tAB[:, 0:1024], tAB[:, 0:1024], zs[:, b, :], ALU.mult, ['tAB', 'zs'], ['tAB'])
                    MS(ssc[:, 0:2], 0.0, ['ssc'], eng='dve')
                    for g in range(2):
                        ACT(tAB[:, 1024 + g * 512:1536 + g * 512], tAB[:, g * 512:(g + 1) * 512], AF.Square, ['tAB', 'ssc'], ['tAB', 'ssc'], accum=ssc[:, g:g + 1])
                    ACT(ssc[:, 2:4], ssc[:, 0:2], AF.Sqrt, ['ssc', 'epscol'], ['ssc'], bias=epscol[:], scale=1.0 / 512)
                    p.op('dve', lambda e: e.reciprocal(out=ssc[:, 2:4], in_=ssc[:, 2:4]), ['ssc'], ['ssc'])
                    TT(ygn[:].rearrange("p (g n) -> p g n", g=2), tAB[:, 0:1024].rearrange("p (g n) -> p g n", g=2),
                       ssc[:, 2:4].unsqueeze(2).to_broadcast([128, 2, 512]), ALU.mult, ['tAB', 'ssc'], ['ygn'])
                    for c in range(8):
                        TR(psTb[:, c * 128:(c + 1) * 128], ygn[:, c * 128:(c + 1) * 128], ident_b[:], ['ygn', 'ident_b'], ['psTb0', 'psTb1'])
                    TT(ysT[:, :, bs], psTb[:].rearrange("p (c n) -> p c n", c=8),
                       P1[l][:, 16:24].unsqueeze(2).to_broadcast([128, 8, 128]), ALU.mult, ['psTb0', 'psTb1', f'P1_{l}'], ['ysT'])
                    ssd_state(l, b, True)

                TAG[0] = 'B_merge'
                PSW[0] = psX
                brT = [(ypT, 'ypT', 4), (atT, 'atT', 4), (ysT, 'ysT', 8)]
                for dg in range(2):
                    for br in range(3):
                        wg, wgn = w_get(l, f'in_g{br}{dg}')
                        wstate['i'] += 1
                        wp, wpn = w_get(l, f'p{br}{dg}')
                        wstate['i'] -= 1
                        src, srcn, kc = brT[br]
                        for d4 in range(4):
                            ps, pn = proj(wg, wgn, d4 * 128, 128, uT, 'uT', 8)
                            g_, gn = gt.next()
                            ACT(g_[:], ps[:], AF.Sigmoid, [pn], [gn])
                            ps2, pn2 = proj(wp, wpn, d4 * 128, 128, src, srcn, kc)
                            acc_ap = Dsb[:, d4 * 512:(d4 + 1) * 512]
                            if br == 0:
                                TT(acc_ap, g_[:], ps2[:], ALU.mult, [gn, pn2], ['Dsb'])
                            else:
                                TT(g_[:], g_[:], ps2[:], ALU.mult, [gn, pn2], [gn])
                                if br == 1:
                                    TT(acc_ap, acc_ap, g_[:], ALU.add, ['Dsb', gn], ['Dsb'])
                                else:
                                    TT(mgT[:, dg * 4 + d4, :], acc_ap, g_[:], ALU.add, ['Dsb', gn], ['xtok'])
                        w_rel()
                        w_rel()
                for blk in range(2):
                    w, wn = w_get(l, f'wo{blk}')
                    for d4 in range(4):
                        ps, pn = proj(w, wn, d4 * 128, 128, mgT, 'xtok', 8)
                        TT(xT[:, blk * 4 + d4, :], xT[:, blk * 4 + d4, :], ps[:], ALU.add, ['xT', pn], ['xT'])
                    w_rel()
                PSW[0] = psA
                store_xT(s)
                if s == NSL - 1:
                    xi, xn = xin.next()
                    tok_major_block(NB - 1, xi, xn)
                    p.dma('sp', HS, xi[112:128, :], reads=[xn], writes=['hs'])
                    load_xT(0)
            p.coll(dict(kind="AllGather", op=ALU.bypass, replica_groups=RG, ins=[HS.opt()], outs=[HG.opt()]), reads=['hs'], writes=['hg'])

            TAG[0] = 'haloC'
            halo_from(l, HG, True)
            norm_small(P1[l], f'P1_{l}', 8)
            for i in range(11):
                w, wn = w_get(l, f'up{i}')
                for r in range(4):
                    kk = 2 * i + (r % 2)
                    cidx = kk if r < 2 else 22 + kk
                    ps, pn = proj16(w, wn, r * 128, 128)
                    CP(fcar[l][:, cidx, :], ps[:, 14:16], [pn], [(f'fcar{l}', cidx)])
                w_rel()

            if l == DEPTH - 1:
                p.dma('sp', PF, I['norm_final'].partition_broadcast(128), writes=['tAB'])
            for s in range(NSL):
                TAG[0] = 'C_ffn'
                if s > 0:
                    load_xT(s)
                rmsnorm_to_uT(P1[l], f'P1_{l}', 8)
                for half in range(2):
                    blks = range(0, 6) if half == 0 else range(6, 11)
                    nk = 12 if half == 0 else 10
                    TAG[0] = 'C_up'
                    PSW[0] = psX
                    pend_silu = []
                    for i in blks:
                        w, wn = w_get(l, f'up{i}')
                        for r in range(4):
                            kk = 2 * i + (r % 2)
                            cidx = kk if r < 2 else 22 + kk
                            ps, pn = proj(w, wn, r * 128, 128, uT, 'uT', 8)
                            wc = [P2[l][:, cidx:cidx + 1], P2[l][:, 44 + cidx:45 + cidx], P3[l][:, cidx:cidx + 1]]
                            a, an = conv_chunk(ps, pn, 3, fcar[l], f'fcar{l}', cidx, None, [f'P2_{l}', f'P3_{l}'], wc, P3[l][:, 44 + cidx:45 + cidx])
                            sgo = 2 * (i % 2)
                            while pend_silu:
                                pend_silu.pop(0)()
                            if r < 2:
                                def do_silu(o_=sgt[sgo + r][0], on_=sgt[sgo + r][1], a=a, an=an):
                                    ACT(o_[:], a[:], AF.Silu, [an], [on_])
                                pend_silu.append(do_silu)
                            else:
                                TT(actT[:, kk - (0 if half == 0 else 12), :], sgt[sgo + r - 2][0][:], a[:], ALU.mult, [sgt[sgo + r - 2][1], an], ['zs', 'EM'], eng='pool')
                        w_rel()
                    TAG[0] = 'C_dn'
                    for dd in range(4):
                        w, wn = w_get(l, f'dn{half}{dd}')
                        for d2 in range(2):
                            d = 2 * dd + d2
                            ps, pn = psA.next()
                            for k in range(nk):
                                MM(ps[:], w[:, k, d2 * 128:(d2 + 1) * 128], actT[:, k, :], k == 0, k == nk - 1, [wn, 'zs', 'EM'], [pn])
                            TT(xT[:, d, :], xT[:, d, :], ps[:], ALU.add, ['xT', pn], ['xT'])
                        w_rel()
                PSW[0] = psA
                if l < DEPTH - 1:
                    store_xT(s)
                    if s == NSL - 1:
                        xi, xn = xin.next()
                        tok_major_block(NB - 1, xi, xn)
                        p.dma('sp', HS, xi[112:128, :], reads=[xn], writes=['hs'])
                else:
                    for b in range(NB):
                        xi, xn = xin.next()
                        tok_major_block(b, xi, xn)
                        MS(ssc[:, 0:1], 0.0, ['ssc'], eng='dve')
                        ACT(tAB[:, 0:1024], xi, AF.Square, [xn, 'ssc'], ['tAB', 'ssc'], accum=ssc[:, 0:1])
                        ACT(ssc[:, 2:3], ssc[:, 0:1], AF.Sqrt, ['ssc', 'epscol'], ['ssc'], bias=epscol[:], scale=1.0 / D)
                        p.op('dve', lambda e: e.reciprocal(out=ssc[:, 2:3], in_=ssc[:, 2:3]), ['ssc'], ['ssc'])
                        STT(xi, xi, ssc[:, 2:3], PF, ALU.mult, ALU.mult, [xn, 'ssc', 'tAB'], [xn])
                        p.dma('sp', Y[s * T + b * 128:s * T + (b + 1) * 128, :], xi, reads=[xn])
            if l < DEPTH - 1:
                p.coll(dict(kind="AllGather", op=ALU.bypass, replica_groups=RG, ins=[HS.opt()], outs=[HG.opt()]), reads=['hs'], writes=['hg'])
        p.finish('sp')
        p.emit()
    return nc, p


_CACHE = {}


def make_corr():
    corr = np.ones((128, 64), np.float32)
    for g in range(4):
        w = 2 << g
        for t in range(16):
            corr[:, g * 16 + t] = w / min(t + 1, w)
    return corr


def kernel(**inputs):
    x = np.asarray(inputs['x'], np.float32)
    B, S, _ = x.shape
    SC = S // 4
    if SC not in _CACHE:
        _CACHE[SC] = build(SC)
    nc, _ = _CACHE[SC]
    n = 8
    in_maps = []
    for i in range(n):
        b, r = i // 4, i % 4
        rk = np.zeros((128, 16), np.float32)
        for q in range(4):
            rk[:, q] = 1.0 if q < r else 0.0
            rk[:, 4 + q] = 0.0 if q < r else -30000.0
            rk[:, 8 + q] = 1.0 if q == r - 1 else 0.0
            rk[:, 12 + q] = 0.0 if q < r else 1.0
        xh0 = np.zeros((16, D), np.float32) if r == 0 else np.ascontiguousarray(x[b, r * SC - 16:r * SC])
        m = {'x': np.ascontiguousarray(x[b, r * SC:(r + 1) * SC]), 'xh0': xh0, 'rk': rk,
             'corr': make_corr() if r == 0 else np.ones((128, 64), np.float32)}
        for k in WNAMES[1:]:
            m[k] = np.ascontiguousarray(np.asarray(inputs[k], np.float32))
        in_maps.append(m)
    res = run_bass_kernel_spmd(nc, in_maps, core_ids=list(range(n)))
    out = np.stack([np.concatenate([res.results[b * 4 + r]['y'] for r in range(4)], axis=0) for b in range(B)], axis=0)
    return out.astype(np.float32)
```
